# Optimizing a Trainium2 kernel written in Bass

```python
import math
import jax, jax.numpy as jnp
from jax import lax
import numpy as np

D_MODEL = 2048
BATCH = 2
SEQ = 4096
DEPTH = 2
DEC_BATCH = 4
DEC_SEQ = 2048
PAST_LEN = 128

N_MIXERS = 2
N_POOL_LAYERS = (DEPTH + 1) // 2
N_ATTN_LAYERS = DEPTH // 2
POOL_WINDOWS = (2, 4, 8, 16)
N_POOL_GROUPS = len(POOL_WINDOWS)
POOL_GC = D_MODEL // N_POOL_GROUPS
HEAD_DIM = 128
N_HEADS = D_MODEL // HEAD_DIM
N_KV_HEADS = N_HEADS // 4
GQA_GROUP = N_HEADS // N_KV_HEADS
QKV_DIM = (N_HEADS + 2 * N_KV_HEADS) * HEAD_DIM
AXIS_DIM = HEAD_DIM // 2
ROPE_THETA = 10000.0
GRID_W = 64
Q_BLOCK = 128
D_FF = 5632
N_MOD = 9
EPS = 1e-6

kernel_name = "hybrid_pool_gqa_macaron_adaln_encoder"


def _rmsnorm(x, g):
    xf = x.astype(jnp.float32)
    y = xf * lax.rsqrt(jnp.mean(xf * xf, axis=-1, keepdims=True) + EPS)
    return (y * g.astype(jnp.float32)).astype(x.dtype)


def _swiglu(h, w_in, w_out):
    gu = h @ w_in
    g, u = jnp.split(gu, 2, axis=-1)
    return (jax.nn.silu(g) * u) @ w_out


def _pool_mixer(h, w_grp, scale):
    B, S, D = h.shape
    hf = h.astype(jnp.float32)
    cs = jnp.concatenate([jnp.zeros((B, 1, D), jnp.float32), jnp.cumsum(hf, axis=1)], axis=1)
    t = jnp.arange(S)
    outs = []
    for gi, w in enumerate(POOL_WINDOWS):
        lo = jnp.clip(t - w // 2, 0, S)
        hi = jnp.clip(t + w // 2, 0, S)
        csg = cs[..., gi * POOL_GC:(gi + 1) * POOL_GC]
        win_sum = jnp.take(csg, hi, axis=1) - jnp.take(csg, lo, axis=1)
        cnt = (hi - lo).astype(jnp.float32)[None, :, None]
        outs.append(win_sum / cnt - hf[..., gi * POOL_GC:(gi + 1) * POOL_GC])
    p = jnp.stack(outs, axis=2).astype(h.dtype)
    y = jnp.einsum('bsgc,gcd->bsgd', p, w_grp).reshape(B, S, D)
    return y * scale


def _axial_rope_tables(S):
    rows = S // GRID_W
    r = jnp.repeat(jnp.arange(rows), GRID_W).astype(jnp.float32)
    c = jnp.tile(jnp.arange(GRID_W), rows).astype(jnp.float32)
    inv = ROPE_THETA ** (-jnp.arange(0, AXIS_DIM, 2, dtype=jnp.float32) / AXIS_DIM)
    ang_r = r[:, None] * inv[None, :]
    ang_c = c[:, None] * inv[None, :]
    return jnp.cos(ang_r), jnp.sin(ang_r), jnp.cos(ang_c), jnp.sin(ang_c)


def _rotate(x, cos, sin):
    x1, x2 = jnp.split(x, 2, axis=-1)
    cs = cos[None, :, None, :]
    sn = sin[None, :, None, :]
    return jnp.concatenate([x1 * cs - x2 * sn, x1 * sn + x2 * cs], axis=-1)


def _apply_axial_rope(x, tables):
    cr, sr, cc, sc = tables
    xf = x.astype(jnp.float32)
    xr = _rotate(xf[..., :AXIS_DIM], cr, sr)
    xc = _rotate(xf[..., AXIS_DIM:], cc, sc)
    return jnp.concatenate([xr, xc], axis=-1).astype(x.dtype)


def _attn_mixer(h, w_qkv, q_g, k_g, w_o):
    B, S, D = h.shape
    qkv = h @ w_qkv
    q = qkv[..., :N_HEADS * HEAD_DIM].reshape(B, S, N_HEADS, HEAD_DIM)
    k = qkv[..., N_HEADS * HEAD_DIM:(N_HEADS + N_KV_HEADS) * HEAD_DIM].reshape(B, S, N_KV_HEADS, HEAD_DIM)
    v = qkv[..., (N_HEADS + N_KV_HEADS) * HEAD_DIM:].reshape(B, S, N_KV_HEADS, HEAD_DIM)
    q = _rmsnorm(q, q_g)
    k = _rmsnorm(k, k_g)
    tables = _axial_rope_tables(S)
    q = _apply_axial_rope(q, tables)
    k = _apply_axial_rope(k, tables)
    n_blk = S // Q_BLOCK
    qb = q.reshape(B, n_blk, Q_BLOCK, N_KV_HEADS, GQA_GROUP, HEAD_DIM).transpose(1, 0, 2, 3, 4, 5)
    scale = 1.0 / math.sqrt(HEAD_DIM)

    def one_block(qi):
        s = jnp.einsum('bqkgd,bskd->bkgqs', qi, k, preferred_element_type=jnp.float32) * scale
        p = jax.nn.softmax(s, axis=-1)
        return jnp.einsum('bkgqs,bskd->bqkgd', p.astype(v.dtype), v)

    o = lax.map(one_block, qb)
    o = o.transpose(1, 0, 2, 3, 4, 5).reshape(B, S, N_HEADS * HEAD_DIM)
    return o @ w_o


def _trunk(x, c, ada_w, ada_b, norm_g, ffn_w_in, ffn_w_out, pool_w, pool_scale,
           attn_w_qkv, attn_q_g, attn_k_g, attn_w_o):
    B, S, D = x.shape
    for i in range(DEPTH):
        mod = (jax.nn.silu(c) @ ada_w[i] + ada_b[i]).reshape(B, 3, 3, D)
        shift = mod[:, :, 0, None, :]
        scl = mod[:, :, 1, None, :]
        gate = mod[:, :, 2, None, :]
        h = _rmsnorm(x, norm_g[i, 0]) * (1.0 + scl[:, 0]) + shift[:, 0]
        x = x + 0.5 * gate[:, 0] * _swiglu(h, ffn_w_in[i, 0], ffn_w_out[i, 0])
        h = _rmsnorm(x, norm_g[i, 1]) * (1.0 + scl[:, 1]) + shift[:, 1]
        j = i // N_MIXERS
        if i % N_MIXERS == 0:
            m = _pool_mixer(h, pool_w[j], pool_scale[j])
        else:
            m = _attn_mixer(h, attn_w_qkv[j], attn_q_g[j], attn_k_g[j], attn_w_o[j])
        x = x + gate[:, 1] * m
        h = _rmsnorm(x, norm_g[i, 2]) * (1.0 + scl[:, 2]) + shift[:, 2]
        x = x + 0.5 * gate[:, 2] * _swiglu(h, ffn_w_in[i, 1], ffn_w_out[i, 1])
    return x


def setup_inputs(seed: int = 0) -> dict:
    key = jax.random.key(seed)
    ks = jax.random.split(key, 16)
    f32 = jnp.float32
    nrm = lambda k, shape, s: jax.random.normal(k, shape, f32) * s
    return {
        "x_prompt": nrm(ks[0], (BATCH, SEQ, D_MODEL), 1.0),
        "x_sample": nrm(ks[1], (DEC_BATCH, DEC_SEQ, D_MODEL), 1.0),
        "c_prompt": nrm(ks[2], (BATCH, D_MODEL), 1.0),
        "c_sample": nrm(ks[3], (DEC_BATCH, D_MODEL), 1.0),
        "ada_w": nrm(ks[4], (DEPTH, D_MODEL, N_MOD * D_MODEL), 0.5 * D_MODEL ** -0.5),
        "ada_b": nrm(ks[5], (DEPTH, N_MOD * D_MODEL), 0.02),
        "norm_g": 1.0 + nrm(ks[6], (DEPTH, 3, D_MODEL), 0.05),
        "ffn_w_in": nrm(ks[7], (DEPTH, 2, D_MODEL, 2 * D_FF), D_MODEL ** -0.5),
        "ffn_w_out": nrm(ks[8], (DEPTH, 2, D_FF, D_MODEL), D_FF ** -0.5),
        "pool_w": nrm(ks[9], (N_POOL_LAYERS, N_POOL_GROUPS, POOL_GC, POOL_GC), POOL_GC ** -0.5),
        "pool_scale": 1.0 + nrm(ks[10], (N_POOL_LAYERS, D_MODEL), 0.1),
        "attn_w_qkv": nrm(ks[11], (N_ATTN_LAYERS, D_MODEL, QKV_DIM), D_MODEL ** -0.5),
        "attn_q_g": 1.0 + nrm(ks[12], (N_ATTN_LAYERS, HEAD_DIM), 0.05),
        "attn_k_g": 1.0 + nrm(ks[13], (N_ATTN_LAYERS, HEAD_DIM), 0.05),
        "attn_w_o": nrm(ks[14], (N_ATTN_LAYERS, N_HEADS * HEAD_DIM, D_MODEL), (N_HEADS * HEAD_DIM) ** -0.5),
    }


def reference(x_prompt, x_sample, c_prompt, c_sample, ada_w, ada_b, norm_g, ffn_w_in, ffn_w_out,
              pool_w, pool_scale, attn_w_qkv, attn_q_g, attn_k_g, attn_w_o):
    y_prompt = _trunk(x_prompt, c_prompt, ada_w, ada_b, norm_g, ffn_w_in, ffn_w_out, pool_w, pool_scale,
                      attn_w_qkv, attn_q_g, attn_k_g, attn_w_o)
    y_sample = _trunk(x_sample, c_sample, ada_w, ada_b, norm_g, ffn_w_in, ffn_w_out, pool_w, pool_scale,
                      attn_w_qkv, attn_q_g, attn_k_g, attn_w_o)
    return (y_prompt, y_sample)
```

```python
import math
from contextlib import ExitStack
import numpy as np
import concourse.bass as bass
import concourse.mybir as mybir
from concourse.bass_utils import run_bass_kernel_spmd

F32 = mybir.dt.float32
BF16 = mybir.dt.bfloat16
AF = mybir.ActivationFunctionType
ALU = mybir.AluOpType
EPS = 1e-6
ROPE_THETA = 10000.0
POOL_WINDOWS = (2, 4, 8, 16)
import os
OPT_NORM_OVERLAP = os.environ.get("K_NORM", "1") == "1"
OPT_QK_PIPE = os.environ.get("K_QK", "1") == "1"
OPT_POOL_GPSIMD = os.environ.get("K_POOL", "0") == "1"


class Cfg:
    def __init__(s, D, DFF, NT, TT, TG, NH, NKV, GRID_W, L=2):
        s.D, s.DFF, s.NT, s.TT, s.TG, s.NH, s.NKV, s.GRID_W, s.L = D, DFF, NT, TT, TG, NH, NKV, GRID_W, L
        s.KC = D // 128
        s.NJ = DFF // 128
        s.NJH = s.NJ // 2
        s.NTILE = NT // TT
        s.NTG = TT // TG
        s.NTGF = NT // TG
        s.GC = D // 4
        s.ICG = s.GC // 128
        s.RSUB = 3 * D // 256
        s.NQC = (NH + 1) // 2
        s.NKC = (NKV + 1) // 2
        s.NVC = (NKV + 1) // 2
        s.NB = NT // 128
        s.SEQ_P = 2 * NT
        s.SEQ_S = NT
        assert NH * 128 == D and s.NJ % 2 == 0 and TG <= 512 and TT % TG == 0


CFG_FULL = Cfg(D=2048, DFF=5632, NT=2048, TT=1024, TG=512, NH=16, NKV=4, GRID_W=64)


_LAST_SCHED = [None]


class Buf:
    __slots__ = ("name", "w", "r")

    def __init__(self, name):
        self.name = name
        self.w = None
        self.r = {}


class Sched:
    ENGS = ("pe", "act", "dve", "pool", "sp")

    def __init__(self, nc, es):
        self.nc, self.es = nc, es
        self.q = {e: [] for e in self.ENGS}
        self.tr = {e: [] for e in self.ENGS}
        self.sems = {}
        self.cnt = {}
        self.waited = {}
        for e in ("pe", "act", "dve", "pool"):
            self.sem("e_" + e)

    def sem(self, name):
        h = self.es.enter_context(self.nc.semaphore(name))
        self.sems[name] = h
        self.cnt[name] = 0
        return name

    def _deps(self, eng, reads, writes, extra=()):
        best = {}

        def add(ev):
            if ev is None:
                return
            k, v = ev
            if best.get(k, 0) < v:
                best[k] = v
        for b in reads:
            add(b.w)
        for b in writes:
            add(b.w)
            for k, v in b.r.items():
                add((k, v))
        for ev in extra:
            add(ev)
        for k, v in best.items():
            if eng == "pe" and k == "e_pe":
                continue
            key = (eng, k)
            if self.waited.get(key, 0) < v:
                self.waited[key] = v
                h = self.sems[k]
                self.tr[eng].append(("wait", k, v))
                self.q[eng].append(lambda E, h=h, v=v: E.wait_ge(h, v))

    def _mark(self, ev, reads, writes):
        for b in writes:
            b.w = ev
            b.r = {}
        for b in reads:
            k, v = ev
            if b.r.get(k, 0) < v:
                b.r[k] = v

    def op(self, eng, reads, writes, emit, extra=()):
        self._deps(eng, reads, writes, extra)
        k = "e_" + eng
        self.cnt[k] += 1
        ev = (k, self.cnt[k])
        h = self.sems[k]
        self.tr[eng].append(("inc", k, 1))
        self.q[eng].append(lambda E, h=h: emit(E).then_inc(h, 1))
        self._mark(ev, reads, writes)
        return ev

    def dma(self, queue, dsem, out_ap, in_ap, reads, writes, extra=(), **kw):
        self._deps(queue, reads, writes, extra)
        self.cnt[dsem] += 16
        ev = (dsem, self.cnt[dsem])
        h = self.sems[dsem]
        self.tr[queue].append(("inc", dsem, 16))
        self.q[queue].append(lambda E, h=h: E.dma_start(out=out_ap, in_=in_ap, **kw).then_inc(h, 16))
        self._mark(ev, reads, writes)
        return ev

    def custom(self, queue, dsem, inc, reads, writes, emit):
        self._deps(queue, reads, writes)
        self.cnt[dsem] += inc
        ev = (dsem, self.cnt[dsem])
        h = self.sems[dsem]
        self.tr[queue].append(("inc", dsem, inc))
        self.q[queue].append(lambda E, h=h: emit(E).then_inc(h, inc))
        self._mark(ev, reads, writes)
        return ev

    def wait_all(self, eng, bufs):
        self._deps(eng, bufs, bufs)


def fence_events(bufs):
    evs = []
    for b in bufs:
        if b.w is not None:
            evs.append(b.w)
        evs.extend(b.r.items())
    return evs


class Ring:
    def __init__(self, S, name, aps):
        self.S = S
        self.aps = aps
        self.bufs = [Buf(f"{name}{i}") for i in range(len(aps))]
        self.sems = [S.sem(f"d_{name}{i}") for i in range(len(aps))]
        self.n = 0

    def next(self):
        i = self.n % len(self.aps)
        self.n += 1
        return self.aps[i], self.bufs[i], self.sems[i]


def build_program(cfg):
    D, KC, NT, TT, TG, NTG, NTILE = cfg.D, cfg.KC, cfg.NT, cfg.TT, cfg.TG, cfg.NTG, cfg.NTILE
    NJ, NJH, NH, NKV, L = cfg.NJ, cfg.NJH, cfg.NH, cfg.NKV, cfg.L
    GQ = NH // NKV
    ICG, GC, NB = cfg.ICG, cfg.GC, cfg.NB
    NTGF = cfg.NTGF
    RSUB = cfg.RSUB
    NSUB = 3 * L
    SM_SCALE = 1.0 / math.sqrt(128.0)

    nc = bass.Bass("TRN2", target_bir_lowering=False)

    def din(name, shape, dt=F32):
        return nc.dram_tensor(name, list(shape), dt, kind="ExternalInput").ap()

    xT = din("xT", [D, NT])
    yT = nc.dram_tensor("yT", [D, NT], F32, kind="ExternalOutput").ap()
    cvec_d = din("cvec", [128, KC])
    adaw_d = din("ada_w", [L, 3 * RSUB, 128, KC * 256])
    adab_d = din("ada_b", [128, NSUB * 3 * KC])
    normg_d = din("norm_g", [128, NSUB * KC])
    win_d = din("w_in", [L, 2, NJ, 128, KC * 256])
    wout_d = din("w_out", [L, 2, 2, KC, 128, NJH * 128])
    poolw_d = din("pool_w", [4, 128, ICG * GC])
    pscale_d = din("pool_scale", [128, KC])
    wq_d = din("w_q", [cfg.NQC, 128, KC * 256])
    wk_d = din("w_k", [cfg.NKC, 128, KC * 256])
    wv_d = din("w_v", [cfg.NVC, 128, KC * 256])
    qkg_d = din("qkg", [128, 2])
    wo_d = din("w_o", [KC, 128, NH * 128])
    cos_d = din("cosT", [128, NT])
    sin_d = din("sinS", [128, NT])
    pt_d = din("PT", [128, 128])
    invc_d = din("invc", [128, 64])
    masks_d = din("masks", [128, 4])

    halo_out = nc.dram_tensor("halo_out", [128, KC * 16], F32)
    halo_in = nc.dram_tensor("halo_in", [2 * 128, KC * 16], F32)
    k_out = nc.dram_tensor("k_out", [NKV * 128, NT], BF16)
    k_in = nc.dram_tensor("k_in", [2 * NKV * 128, NT], BF16)
    v_out = nc.dram_tensor("v_out", [NT, NKV * 128], BF16)
    v_in = nc.dram_tensor("v_in", [2 * NT, NKV * 128], BF16)

    xT3 = xT.rearrange("(c p) t -> c p t", p=128)
    yT3 = yT.rearrange("(c p) t -> c p t", p=128)

    es = ExitStack()
    with es:
        def sb(name, shape, dt):
            return es.enter_context(nc.sbuf_tensor("sb_" + name, list(shape), dt))

        S = Sched(nc, es)

        h_t = sb("h", [128, KC, TT], BF16)
        r_ffn = NJH * TT
        LP = NT + 16
        r_pool = 8 * LP + 2 * ICG * NT
        r_attn = max(NKV * NT + NB * NKV * 128, NH * TT + 2 * (2 * NT + 2 * NB * 128))
        R_EL = max(r_ffn, r_pool, r_attn)
        R_t = sb("R", [128, R_EL], BF16)
        WA_EL = KC * 256
        NSA = 3
        ringA_t = sb("ringA", [128, NSA, WA_EL], BF16)
        WB_EL = max(NJH, NH) * 128
        NSB = 2
        ringB_t = sb("ringB", [128, NSB, WB_EL], BF16)
        NXS = 4
        xst_t = sb("xstage", [128, NXS, TG], F32)
        NXO = 3
        xout_t = sb("xout", [128, NXO, TG], F32)
        rstd_t = sb("rstd", [128, NT], F32)
        NSQ = 3
        sq_t = sb("sq", [128, NSQ, TG], BF16)
        NSG = 2
        NE = 3
        e_t = sb("ebuf", [128, NE, 2 * TG], BF16)
        tabc_t = sb("tabc", [128, TT], F32)
        tabs_t = sb("tabs", [128, TT], F32)
        NQT = 2
        u_t = sb("qu", [128, NQT, TG], F32)
        sg_t = u_t
        t1_t = sb("qt1", [128, NQT, TG], F32)
        t2_t = sb("qt2", [128, NQT, TG], F32)
        rq_t = sb("qrq", [128, NQT, TG], F32)
        rz_t = sb("rz", [128, 2, TG], F32)
        zt_t = sb("zt", [128, 2, TG], F32)
        zs_t = sb("zs", [128, 2, TG], BF16)
        epsc_t = sb("epsc", [128, 1], F32)
        onesD_t = sb("onesD", [128, 128], BF16)
        onesH_t = sb("onesH", [128, 128], BF16)
        ones1_t = sb("ones1", [128, 128], BF16)
        pt_t = sb("pt", [128, 128], F32)
        cvec_t = sb("cvec", [128, KC], F32)
        sc_t = sb("sc", [128, KC], BF16)
        adab_t = sb("adab", [128, NSUB * 3 * KC], F32)
        normg_t = sb("normg", [128, NSUB * KC], F32)
        pscale_t = sb("pscale", [128, KC], F32)
        qkg_t = sb("qkg", [128, 2], F32)
        invc_t = sb("invc", [128, 64], F32)
        masks_t = sb("masks", [128, 4], F32)
        modraw_t = sb("modraw", [128, 3 * KC], F32)
        modA_t = sb("modA", [128, NSUB, KC], F32)
        modB_t = sb("modB", [128, NSUB, KC], F32)
        modG_t = sb("modG", [128, NSUB, KC], F32)
        xb_t = sb("xb", [128, KC, 16], F32)
        re_t = sb("re", [128, 16], F32)
        hal_t = sb("hal", [128, 2, KC, 16], F32)
        halL_t = sb("halL", [128, KC, 8], F32)
        halR_t = sb("halR", [128, KC, 8], F32)
        e16_t = sb("e16", [128, 2, 8], F32)

        psum_all = es.enter_context(nc.psum_tensor("psum_all", [128, 8 * 512], F32))
        banks = [psum_all[:, i * 512:(i + 1) * 512] for i in range(8)]
        bankb = [Buf(f"bank{i}") for i in range(8)]

        h_b = [Buf(f"h{c}") for c in range(KC)]
        rstd_b = [Buf(f"rstd{t}") for t in range(NTILE)]
        consts_b = Buf("consts")
        sc_b = Buf("sc")
        mods_b = [Buf(f"mods{i}") for i in range(NSUB)]
        modsG_b = [Buf(f"modsG{i}") for i in range(NSUB)]
        modraw_b = Buf("modraw")
        tab_b = Buf("tab")
        y_b = [[Buf(f"y{c}_{g}") for g in range(NTGF)] for c in range(KC)]

        ringA = Ring(S, "A", [ringA_t[:, i, :] for i in range(NSA)])
        ringB = Ring(S, "B", [ringB_t[:, i, :] for i in range(NSB)])
        xst = Ring(S, "xs", [xst_t[:, i, :] for i in range(NXS)])
        xout_aps = [xout_t[:, i, :] for i in range(NXO)]
        xout_b = [Buf(f"xo{i}") for i in range(NXO)]
        xout_s = [S.sem(f"d_xo{i}") for i in range(NXO)]
        xout_n = [0]
        sq_b = [Buf(f"sq{i}") for i in range(NSQ)]
        sq_n = [0]
        sg_b = None
        sg_n = [0]
        e_b = [Buf(f"e{i}") for i in range(NE)]
        e_n = [0]
        qt_b = [[Buf(f"q{n}{i}") for i in range(NQT)] for n in ("u", "t1", "t2", "rq")]
        qt_n = [0]
        sg_b = qt_b[0][:NSG]
        rz_b = [Buf("rz0"), Buf("rz1")]
        rz_n = [0]
        zt_b = [Buf("zt0"), Buf("zt1")]
        zs_b = [Buf("zs0"), Buf("zs1")]
        zt_n = [0]
        zq_pend = []

        s_init = S.sem("d_init")
        s_misc = S.sem("d_misc")
        s_cc = [S.sem(f"d_cc{i}") for i in range(3)]
        s_tab = S.sem("d_tab")
        s_hh1 = S.sem("d_hh1")
        s_kv = [S.sem("d_kv0"), S.sem("d_kv1")]

        R_last = [[]]

        def r_claim(names):
            evs = fence_events(R_last[0])
            bufs = []
            for n in names:
                b = Buf(n)
                for k, v in evs:
                    if b.r.get(k, 0) < v:
                        b.r[k] = v
                bufs.append(b)
            R_last[0] = bufs
            return bufs

        init_loads = [(cvec_t[:], cvec_d), (adab_t[:], adab_d), (normg_t[:], normg_d), (pscale_t[:], pscale_d),
                      (qkg_t[:], qkg_d), (invc_t[:], invc_d), (masks_t[:], masks_d), (pt_t[:], pt_d)]
        for o, i_ in init_loads:
            ev = S.dma("sp", s_init, o, i_, [], [consts_b])
        S.op("dve", [], [consts_b], lambda E: (E.memset(onesD_t[:], 1.0 / D), E.memset(onesH_t[:], 1.0 / 128.0),
                                                E.memset(epsc_t[:], EPS), E.memset(ones1_t[:], 1.0))[-1], extra=[ev])
        consts_b.r = {}
        S.op("act", [consts_b], [sc_b], lambda E: E.activation(out=sc_t[:], in_=cvec_t[:], func=AF.Silu))

        pending_mods = []

        def schedule_mods(ls):
            i, s_ = divmod(ls, 3)
            bk, bkb = banks[7], bankb[7]

            def chunk(r):
                def run():
                    slot, sbuf_, ssem = ringA.next()
                    S.dma("pool", ssem, slot, adaw_d[i, s_ * RSUB + r], [], [sbuf_], max_dma_last_dim=8192)
                    s3 = slot.rearrange("p (k c) -> p k c", k=KC)

                    def emit(E):
                        ins = None
                        for hf in range(2):
                            oc = r * 2 + hf
                            for kc in range(KC):
                                ins = E.matmul(bk[:, oc:oc + 1], lhsT=s3[:, kc, hf * 128:(hf + 1) * 128],
                                               rhs=sc_t[:, kc:kc + 1], start=(kc == 0), stop=(kc == KC - 1))
                        return ins
                    S.op("pe", [sbuf_, sc_b], [bkb], emit)
                    base = ls * 3 * KC + 2 * r
                    S.op("dve", [bkb, consts_b], [modraw_b],
                         lambda E: E.tensor_tensor(out=modraw_t[:, 2 * r:2 * r + 2], in0=bk[:, 2 * r:2 * r + 2],
                                                   in1=adab_t[:, base:base + 2], op=ALU.add))
                return run

            def fin_ab():
                def emit(E):
                    E.tensor_copy(out=modB_t[:, ls, :], in_=modraw_t[:, 0:KC])
                    return E.scalar_tensor_tensor(out=modA_t[:, ls, :], in0=modraw_t[:, KC:2 * KC], scalar=1.0,
                                                  in1=normg_t[:, ls * KC:(ls + 1) * KC], op0=ALU.add, op1=ALU.mult)
                S.op("dve", [modraw_b, consts_b], [mods_b[ls]], emit)

            def fin_g():
                kind = ls % 3

                def emit(E):
                    if kind == 1 and (ls // 3) % 2 == 0:
                        return E.tensor_tensor(out=modG_t[:, ls, :], in0=modraw_t[:, 2 * KC:3 * KC], in1=pscale_t[:],
                                               op=ALU.mult)
                    coef = 1.0 if kind == 1 else 0.5
                    return E.tensor_scalar(out=modG_t[:, ls, :], in0=modraw_t[:, 2 * KC:3 * KC], scalar1=coef,
                                           scalar2=None, op0=ALU.mult)
                S.op("dve", [modraw_b, consts_b], [modsG_b[ls]], emit)
            nab = 2 * RSUB // 3
            for r in range(nab):
                pending_mods.append(chunk(r))
            pending_mods.append(fin_ab)
            for r in range(nab, RSUB):
                pending_mods.append(chunk(r))
            pending_mods.append(fin_g)

        def pump_mods(n=None):
            k = 0
            while pending_mods and (n is None or k < n):
                pending_mods.pop(0)()
                k += 1

        written = [[False] * NTGF for _ in range(KC)]

        def x_src(c, gf):
            return (yT3 if written[c][gf] else xT3)[c][:, gf * TG:(gf + 1) * TG]

        def load_x(c, gf):
            slot, sbuf_, ssem = xst.next()
            S.dma("sp", ssem, slot, x_src(c, gf), [y_b[c][gf]], [sbuf_])
            return slot, sbuf_

        stat_pend = []

        def stat_flush():
            while stat_pend:
                stat_pend.pop(0)()

        def store_x(c, gf, bank_i, G_ap, xold, xold_b, extra_reads=(), stat=None):
            k = xout_n[0] % NXO
            xout_n[0] += 1
            xo, xob = xout_aps[k], xout_b[k]
            S.op("dve", [bankb[bank_i], xold_b, *extra_reads], [xob],
                 lambda E: E.scalar_tensor_tensor(out=xo, in0=banks[bank_i][:, 0:TG], scalar=G_ap, in1=xold,
                                                  op0=ALU.mult, op1=ALU.add))
            S.dma("act", xout_s[k], yT3[c][:, gf * TG:(gf + 1) * TG], xo, [xob], [y_b[c][gf]])
            written[c][gf] = True
            if stat is not None:
                sbk, first, last = stat
                kq = sq_n[0] % NSQ
                sq_n[0] += 1
                sqa, sqb = sq_t[:, kq, :], sq_b[kq]
                S.op("act", [xob], [sqb], lambda E: E.activation(out=sqa, in_=xo, func=AF.Square))
                stat_flush()
                stat_pend.append(lambda: S.op(
                    "pe", [sqb, consts_b], [bankb[sbk]],
                    lambda E: E.matmul(banks[sbk][:, 0:TG], lhsT=onesD_t[:], rhs=sqa, start=first, stop=last)))

        def stat_fin(pairs):
            stat_flush()
            for sbk, gf in pairs:
                dst = rstd_t[:, gf * TG:(gf + 1) * TG]
                rb = rstd_b[gf // NTG]
                S.op("act", [bankb[sbk], consts_b], [rb],
                     lambda E, sbk=sbk, dst=dst: E.activation(out=dst, in_=banks[sbk][:, 0:TG], func=AF.Ln,
                                                              bias=epsc_t[:, 0:1], scale=1.0))
                S.op("act", [rb], [rb], lambda E, dst=dst: E.activation(out=dst, in_=dst, func=AF.Exp, bias=0.0, scale=-0.5))

        def norm_stats_items(tile, rstd_ap, rstd_buf, bank_ids):
            items = []
            pend = []
            for c in range(KC):
                for tg in range(NTG):
                    def item(c=c, tg=tg):
                        gf = tile * NTG + tg
                        xs, xsb = load_x(c, gf)
                        k = sq_n[0] % NSQ
                        sq_n[0] += 1
                        sqa, sqb = sq_t[:, k, :], sq_b[k]
                        S.op("act", [xsb], [sqb], lambda E: E.activation(out=sqa, in_=xs, func=AF.Square))
                        bi = bank_ids[tg]
                        while pend:
                            pend.pop(0)()
                        pend.append(lambda: S.op(
                            "pe", [sqb, consts_b], [bankb[bi]],
                            lambda E: E.matmul(banks[bi][:, 0:TG], lhsT=onesD_t[:], rhs=sqa,
                                               start=(c == 0), stop=(c == KC - 1))))
                    items.append(item)

            def fin():
                while pend:
                    pend.pop(0)()
                for tg in range(NTG):
                    bi = bank_ids[tg]
                    dst = rstd_ap[:, tg * TG:(tg + 1) * TG]
                    S.op("act", [bankb[bi]], [rstd_buf],
                         lambda E, bi=bi, dst=dst: E.activation(out=dst, in_=banks[bi][:, 0:TG], func=AF.Sqrt, bias=EPS,
                                                                scale=1.0))
                    S.op("dve", [rstd_buf], [rstd_buf], lambda E, dst=dst: E.reciprocal(out=dst, in_=dst))
            return items, fin

        def norm_stats(tile, rstd_ap, rstd_buf, bank_ids):
            items, fin = norm_stats_items(tile, rstd_ap, rstd_buf, bank_ids)
            for it in items:
                it()
            fin()

        def norm_h_items(ls, tile):
            items = []
            for c in range(KC):
                for tg in range(NTG):
                    def item(c=c, tg=tg):
                        gf = tile * NTG + tg
                        xs, xsb = load_x(c, gf)
                        S.op("dve", [xsb, rstd_b[tile]], [xsb],
                             lambda E: E.tensor_tensor(out=xs, in0=xs, in1=rstd_t[:, gf * TG:(gf + 1) * TG], op=ALU.mult))
                        S.op("act", [xsb, mods_b[ls]], [h_b[c]],
                             lambda E: E.activation(out=h_t[:, c, tg * TG:(tg + 1) * TG], in_=xs, func=AF.Identity,
                                                    bias=modB_t[:, ls, c:c + 1], scale=modA_t[:, ls, c:c + 1]))
                    items.append(item)
            return items

        prenormed = [None]

        def norm_h(ls, tile):
            if prenormed[0] == (ls, tile):
                prenormed[0] = None
                return
            if ls == 0:
                norm_stats(tile, rstd_t[:, tile * TT:(tile + 1) * TT], rstd_b[tile], [6, 7])
            for it in norm_h_items(ls, tile):
                it()

        def next_unit(ls, tile):
            if tile + 1 < NTILE:
                return (ls, tile + 1)
            nls = ls + 1
            if nls >= NSUB:
                return None
            kind, layer = nls % 3, nls // 3
            if kind in (0, 2) or layer % 2 == 1:
                return (nls, 0)
            return None

        def ffn(ls, i, which):
            a_bufs = r_claim([f"a{j}_{g}" for j in range(NJH) for g in range(NTG)])
            a_t = R_t[:, 0:NJH * TT].rearrange("p (j t) -> p j t", j=NJH)

            def ab(jj, tg):
                return a_bufs[jj * NTG + tg]
            for tile in range(NTILE):
                norm_h(ls, tile)
                nxt = next_unit(ls, tile) if OPT_NORM_OVERLAP else None
                if nxt is not None and nxt[0] != ls:
                    pump_mods()
                for hh in range(2):
                    side, side_fin = [], None
                    if nxt is not None:
                        if hh == 0:
                            if ls == 0 and nxt[0] == 0:
                                side, side_fin = norm_stats_items(nxt[1], rstd_t[:, nxt[1] * TT:(nxt[1] + 1) * TT],
                                                                  rstd_b[nxt[1]], [6, 7])
                        else:
                            side = norm_h_items(nxt[0], nxt[1])
                    for jj in range(NJH):
                        j = hh * NJH + jj
                        slot, sbuf_, ssem = ringA.next()
                        S.dma("pool", ssem, slot, win_d[i, which, j], [], [sbuf_], max_dma_last_dim=8192)
                        s3 = slot.rearrange("p (k c) -> p k c", k=KC)
                        for tg in range(NTG):
                            bg, bu = tg % 2, 2 + tg % 2
                            for (bi, off) in ((bg, 0), (bu, 128)):
                                def emit(E, bi=bi, off=off, tg=tg, s3=s3):
                                    ins = None
                                    for kc in range(KC):
                                        ins = E.matmul(banks[bi][:, 0:TG], lhsT=s3[:, kc, off:off + 128],
                                                       rhs=h_t[:, kc, tg * TG:(tg + 1) * TG], start=(kc == 0),
                                                       stop=(kc == KC - 1))
                                    return ins
                                S.op("pe", [sbuf_, *h_b], [bankb[bi]], emit)
                            k = sg_n[0] % NSG
                            sg_n[0] += 1
                            sga, sgb = sg_t[:, k, :], sg_b[k]
                            S.op("act", [bankb[bg]], [sgb],
                                 lambda E, bg=bg, sga=sga: E.activation(out=sga, in_=banks[bg][:, 0:TG], func=AF.Silu))
                            S.op("dve", [bankb[bu], sgb], [ab(jj, tg)],
                                 lambda E, bu=bu, sga=sga, jj=jj, tg=tg: E.tensor_tensor(
                                     out=a_t[:, jj, tg * TG:(tg + 1) * TG], in0=banks[bu][:, 0:TG], in1=sga, op=ALU.mult))
                        pump_mods(1)
                    for fc in range(KC):
                        slot, sbuf_, ssem = ringB.next()
                        S.dma("pool", ssem, slot[:, 0:NJH * 128], wout_d[i, which, hh, fc], [], [sbuf_],
                              max_dma_last_dim=8192)
                        s3 = slot[:, 0:NJH * 128].rearrange("p (j c) -> p j c", j=NJH)
                        for tg in range(NTG):
                            gf = tile * NTG + tg
                            xs, xsb = load_x(fc, gf)
                            bi = 4 + (fc * NTG + tg) % 2

                            def emit(E, bi=bi, s3=s3, tg=tg):
                                ins = None
                                for jj in range(NJH):
                                    ins = E.matmul(banks[bi][:, 0:TG], lhsT=s3[:, jj, :],
                                                   rhs=a_t[:, jj, tg * TG:(tg + 1) * TG], start=(jj == 0),
                                                   stop=(jj == NJH - 1))
                                return ins
                            S.op("pe", [sbuf_] + [ab(jj, tg) for jj in range(NJH)], [bankb[bi]], emit)
                            st_ = (6 + tg, fc == 0, fc == KC - 1) if (hh == 1 and ls + 1 < NSUB) else None
                            store_x(fc, gf, bi, modG_t[:, ls, fc:fc + 1], xs, xsb, [modsG_b[ls]], stat=st_)
                            if side:
                                side.pop(0)()
                    while side:
                        side.pop(0)()
                    if side_fin is not None:
                        side_fin()
                    if hh == 1 and ls + 1 < NSUB:
                        stat_fin([(6 + tg, tile * NTG + tg) for tg in range(NTG)])
                if nxt is not None:
                    prenormed[0] = nxt

        def pool_mixer(ls, j_):
            names = ["hh0", "hh1", "sA", "sB"] + [f"pb{k}_{ic}" for k in range(2) for ic in range(ICG)]
            hh0_b, hh1_b, sA_b, sB_b, *pb_all = r_claim(names)
            Rf = R_t[:, 0:8 * LP].bitcast(F32)
            rstdP = rstd_t[:, 0:NT]
            hhs = [(Rf[:, 0:LP], hh0_b), (Rf[:, LP:2 * LP], hh1_b)]
            sA = Rf[:, 2 * LP:3 * LP]
            sB = Rf[:, 3 * LP:4 * LP]
            o_pb = 8 * LP
            pbs = [R_t[:, o_pb + k * ICG * NT:o_pb + (k + 1) * ICG * NT].rearrange("p (i t) -> p i t", i=ICG)
                   for k in range(2)]
            pbbs = [pb_all[0:ICG], pb_all[ICG:2 * ICG]]
            A_ = modA_t[:, ls, :]
            B_ = modB_t[:, ls, :]
            s_hh = [s_tab, s_hh1]
            fronted = set()

            def chunk_front(c):
                if c in fronted:
                    return
                fronted.add(c)
                hh, hh_b = hhs[c % 2]
                for gf in range(NTGF):
                    S.dma("sp", s_hh[c % 2], hh[:, 8 + gf * TG:8 + (gf + 1) * TG], x_src(c, gf), [y_b[c][gf]], [hh_b])
                S.op("pool" if OPT_POOL_GPSIMD else "dve", [hh_b, *rstd_b], [hh_b],
                     lambda E: E.tensor_tensor(out=hh[:, 8:8 + NT], in0=hh[:, 8:8 + NT], in1=rstdP, op=ALU.mult))
                S.op("act", [hh_b, mods_b[ls]], [hh_b],
                     lambda E: E.activation(out=hh[:, 8:8 + NT], in_=hh[:, 8:8 + NT], func=AF.Identity,
                                            bias=modB_t[:, ls, c:c + 1], scale=modA_t[:, ls, c:c + 1]))
            xb_b, hal_b, halo_b, e16_b = Buf("xb"), Buf("hal"), Buf("halo"), Buf("e16")
            allx = [y_b[c][g] for c in range(KC) for g in (0, NTGF - 1)]
            src3 = (yT if written[0][0] else xT).rearrange("(c p) t -> p c t", p=128)
            CG = 4
            for c0 in range(0, KC, CG):
                S.dma("sp", s_misc, xb_t[:, c0:c0 + CG, 0:8], src3[:, c0:c0 + CG, 0:8], allx, [xb_b])
                S.dma("sp", s_misc, xb_t[:, c0:c0 + CG, 8:16], src3[:, c0:c0 + CG, NT - 8:NT], allx, [xb_b])

            re_b = Buf("re")

            def emit(E):
                E.tensor_copy(out=re_t[:, 0:8], in_=rstdP[:, 0:8])
                return E.tensor_copy(out=re_t[:, 8:16], in_=rstdP[:, NT - 8:NT])
            S.op("dve", [*rstd_b], [re_b], emit)
            S.op("dve", [xb_b, re_b], [xb_b],
                 lambda E: E.tensor_tensor(out=xb_t[:], in0=xb_t[:], in1=re_t[:].unsqueeze(1).broadcast_to([128, KC, 16]),
                                           op=ALU.mult))
            S.op("dve", [xb_b, mods_b[ls]], [xb_b],
                 lambda E: E.tensor_tensor(out=xb_t[:], in0=xb_t[:], in1=A_.unsqueeze(2).broadcast_to([128, KC, 16]),
                                           op=ALU.mult))
            S.op("dve", [xb_b, mods_b[ls]], [xb_b],
                 lambda E: E.tensor_tensor(out=xb_t[:], in0=xb_t[:], in1=B_.unsqueeze(2).broadcast_to([128, KC, 16]),
                                           op=ALU.add))
            hodram_b, hidram_b = Buf("hodram"), Buf("hidram")
            S.dma("sp", s_misc, halo_out.ap(), xb_t[:].rearrange("p c t -> p (c t)"), [xb_b], [hodram_b])
            S.custom("pool", s_cc[0], 1, [hodram_b], [hidram_b],
                     lambda E: E.collective_compute("AllGather", ALU.bypass,
                                                    replica_groups=[[0, 1], [2, 3], [4, 5], [6, 7]],
                                                    ins=[halo_out.ap().opt()], outs=[halo_in.ap().opt()]))
            chunk_front(0)
            chunk_front(1)
            for s2_ in range(2):
                S.dma("sp", s_misc, hal_t[:, s2_, :, :].rearrange("p c t -> p (c t)"),
                      halo_in.ap()[s2_ * 128:(s2_ + 1) * 128, :], [hidram_b], [hal_b])

            def emit(E):
                E.tensor_scalar(out=halL_t[:], in0=hal_t[:, 0, :, 8:16], scalar1=masks_t[:, 0:1], scalar2=None, op0=ALU.mult)
                return E.tensor_scalar(out=halR_t[:], in0=hal_t[:, 1, :, 0:8], scalar1=masks_t[:, 1:2], scalar2=None,
                                       op0=ALU.mult)
            S.op("dve", [hal_b, consts_b], [halo_b], emit)
            s_hh = [s_tab, s_hh1]
            def group_mm_items(gi):
                pb, pb_b = pbs[gi % 2], pbbs[gi % 2]
                slot, sbuf_, ssem = ringA.next()
                S.dma("pool", ssem, slot[:, 0:ICG * GC], poolw_d[gi], [], [sbuf_], max_dma_last_dim=8192)
                s3 = slot[:, 0:ICG * GC].rearrange("p (i c) -> p i c", i=ICG)
                items = []
                for oc in range(ICG):
                    for gf in range(NTGF):
                        def item(oc=oc, gf=gf):
                            fc = gi * ICG + oc
                            xs, xsb = load_x(fc, gf)
                            bi = 4 + (oc * NTGF + gf) % 4

                            def emit(E):
                                ins = None
                                for ic in range(ICG):
                                    ins = E.matmul(banks[bi][:, 0:TG], lhsT=s3[:, ic, oc * 128:(oc + 1) * 128],
                                                   rhs=pb[:, ic, gf * TG:(gf + 1) * TG], start=(ic == 0),
                                                   stop=(ic == ICG - 1))
                                return ins
                            S.op("pe", [sbuf_, *pb_b], [bankb[bi]], emit)
                            store_x(fc, gf, bi, modG_t[:, ls, fc:fc + 1], xs, xsb, [modsG_b[ls]],
                                    stat=(gf, fc == 0, fc == KC - 1))
                        items.append(item)
                return items

            mm_items = []

            def mm_step():
                if mm_items:
                    mm_items.pop(0)()

            for gi, w in enumerate(POOL_WINDOWS):
                pb, pb_b = pbs[gi % 2], pbbs[gi % 2]
                for ic in range(ICG):
                    c = gi * ICG + ic
                    hh, hh_b = hhs[c % 2]
                    chunk_front(c)

                    def emit(E, c=c, hh=hh):
                        E.tensor_copy(out=hh[:, 0:8], in_=halL_t[:, c, :])
                        return E.tensor_copy(out=hh[:, 8 + NT:16 + NT], in_=halR_t[:, c, :])
                    S.op("dve", [hh_b, halo_b], [hh_b], emit)
                    mm_step()
                    cur, curb, width, step = hh, hh_b, LP, 1
                    dsts = [(sA, sA_b), (sB, sB_b)]
                    di = 0
                    while step < w:
                        dst, dstb = dsts[di % 2]
                        di += 1
                        nw = width - step
                        S.op("dve", [curb], [dstb],
                             lambda E, cur=cur, dst=dst, nw=nw, step=step: E.tensor_tensor(
                                 out=dst[:, 0:nw], in0=cur[:, 0:nw], in1=cur[:, step:step + nw], op=ALU.add))
                        mm_step()
                        cur, curb, width, step = dst, dstb, nw, step * 2
                    o0 = 8 - w // 2
                    S.op("dve", [curb, hh_b], [pb_b[ic]],
                         lambda E, cur=cur, o0=o0, ic=ic, w=w, hh=hh, pb=pb: E.scalar_tensor_tensor(
                             out=pb[:, ic, :], in0=cur[:, o0:o0 + NT], scalar=1.0 / w, in1=hh[:, 8:8 + NT],
                             op0=ALU.mult, op1=ALU.subtract))

                    def emit(E, cur=cur, o0=o0, ic=ic, gi=gi):
                        E.tensor_tensor(out=e16_t[:, 0, :], in0=cur[:, o0:o0 + 8], in1=invc_t[:, gi * 16:gi * 16 + 8], op=ALU.mult)
                        return E.tensor_tensor(out=e16_t[:, 1, :], in0=cur[:, o0 + NT - 8:o0 + NT],
                                               in1=invc_t[:, gi * 16 + 8:gi * 16 + 16], op=ALU.mult)
                    S.op("dve", [curb, consts_b], [e16_b], emit)

                    def emit(E, ic=ic, hh=hh, pb=pb):
                        E.tensor_tensor(out=pb[:, ic, 0:8], in0=e16_t[:, 0, :], in1=hh[:, 8:16], op=ALU.subtract)
                        return E.tensor_tensor(out=pb[:, ic, NT - 8:NT], in0=e16_t[:, 1, :], in1=hh[:, NT:NT + 8],
                                               op=ALU.subtract)
                    S.op("dve", [e16_b, hh_b, pb_b[ic]], [pb_b[ic]], emit)
                    mm_step()
                    mm_step()
                while mm_items:
                    mm_items.pop(0)()
                mm_items.extend(group_mm_items(gi))
            while mm_items:
                mm_items.pop(0)()
            stat_fin([(gf, gf) for gf in range(NTGF)])

        qk_pending = []

        def qk_post(bank_i, gcol, dst_ap, dst_buf, tsl):
            k = sq_n[0] % NSQ
            sq_n[0] += 1
            sqa, sqb = sq_t[:, k, :], sq_b[k]
            n = qt_n[0] % NQT
            m2 = qt_n[0] % 2
            qt_n[0] += 1
            u, t1, t2, rq = u_t[:, n, :], t1_t[:, n, :], t2_t[:, n, :], rq_t[:, n, :]
            ub, t1b, t2b, rqb = (qt_b[x][n] for x in range(4))
            bk = banks[bank_i]
            bm, br = 2 + m2, 4 + m2
            S.op("act", [bankb[bank_i]], [sqb], lambda E: E.activation(out=sqa, in_=bk[:, 0:TG], func=AF.Square))
            S.op("act", [bankb[bank_i], consts_b], [ub],
                 lambda E: E.activation(out=u, in_=bk[:, 0:TG], func=AF.Identity, bias=0.0, scale=qkg_t[:, gcol:gcol + 1]))

            def stage2():
                S.op("pe", [sqb, consts_b], [bankb[bm]],
                     lambda E: E.matmul(banks[bm][:, 0:TG], lhsT=onesH_t[:], rhs=sqa, start=True, stop=True))
                S.op("pe", [ub, consts_b], [bankb[br]],
                     lambda E: E.matmul(banks[br][:, 0:TG], lhsT=pt_t[:], rhs=u, start=True, stop=True))
                S.op("act", [bankb[bm]], [rqb],
                     lambda E: E.activation(out=rq, in_=banks[bm][:, 0:TG], func=AF.Ln, bias=epsc_t[:, 0:1], scale=1.0))
                S.op("act", [rqb], [rqb], lambda E: E.activation(out=rq, in_=rq, func=AF.Exp, bias=0.0, scale=-0.5))
                S.op("dve", [ub, tab_b], [t1b], lambda E: E.tensor_tensor(out=t1, in0=u, in1=tabc_t[:, tsl], op=ALU.mult))
                S.op("dve", [bankb[br], tab_b], [t2b],
                     lambda E: E.tensor_tensor(out=t2, in0=banks[br][:, 0:TG], in1=tabs_t[:, tsl], op=ALU.mult))
                S.op("dve", [t1b, t2b], [t1b], lambda E: E.tensor_tensor(out=t1, in0=t1, in1=t2, op=ALU.add))
                S.op("dve", [t1b, rqb], [dst_buf], lambda E: E.tensor_tensor(out=dst_ap, in0=t1, in1=rq, op=ALU.mult))
            while len(qk_pending) >= 1:
                qk_pending.pop(0)()
            qk_pending.append(stage2)
            if not OPT_QK_PIPE:
                qk_flush()

        def qk_flush():
            while qk_pending:
                qk_pending.pop(0)()

        def load_tables(tile):
            qk_flush()
            S.dma("sp", s_tab, tabc_t[:], cos_d[:, tile * TT:(tile + 1) * TT], [], [tab_b])
            S.dma("sp", s_tab, tabs_t[:], sin_d[:, tile * TT:(tile + 1) * TT], [], [tab_b])

        def proj_fm(s3, off, tg, bi, sbuf_):
            def emit(E):
                ins = None
                for kc in range(KC):
                    ins = E.matmul(banks[bi][:, 0:TG], lhsT=s3[:, kc, off:off + 128],
                                   rhs=h_t[:, kc, tg * TG:(tg + 1) * TG], start=(kc == 0), stop=(kc == KC - 1))
                return ins
            S.op("pe", [sbuf_, *h_b], [bankb[bi]], emit)

        def attention(ls, j_):
            kst_b, vst_b = r_claim(["kstage", "vstage"])
            kst = R_t[:, 0:NKV * NT].rearrange("p (k t) -> p k t", k=NKV)
            vst = R_t[:, NKV * NT:NKV * NT + NB * NKV * 128].rearrange("p (b e) -> p b e", b=NB)
            pj = [0]
            for tile in range(NTILE):
                norm_h(ls, tile)
                load_tables(tile)
                for rk in range(cfg.NKC):
                    slot, sbuf_, ssem = ringA.next()
                    S.dma("pool", ssem, slot, wk_d[rk], [], [sbuf_], max_dma_last_dim=8192)
                    s3 = slot.rearrange("p (k c) -> p k c", k=KC)
                    for hd in range(min(2, NKV - rk * 2)):
                        kh = rk * 2 + hd
                        for tg in range(NTG):
                            bi = pj[0] % 2
                            pj[0] += 1
                            proj_fm(s3, hd * 128, tg, bi, sbuf_)
                            t0 = tile * TT + tg * TG
                            qk_post(bi, 1, kst[:, kh, t0:t0 + TG], kst_b, slice(tg * TG, (tg + 1) * TG))
                qk_flush()
                for rv in range(cfg.NVC):
                    slot, sbuf_, ssem = ringA.next()
                    S.dma("pool", ssem, slot, wv_d[rv], [], [sbuf_], max_dma_last_dim=8192)
                    s3 = slot.rearrange("p (k c) -> p k c", k=KC)
                    ncol = min(2, NKV - rv * 2) * 128
                    for tb in range(TT // 128):
                        bi = pj[0] % 2
                        pj[0] += 1

                        def emit(E, bi=bi, s3=s3, tb=tb, ncol=ncol):
                            ins = None
                            for kc in range(KC):
                                ins = E.matmul(banks[bi][:, 0:ncol], lhsT=h_t[:, kc, tb * 128:(tb + 1) * 128],
                                               rhs=s3[:, kc, 0:ncol], start=(kc == 0), stop=(kc == KC - 1))
                            return ins
                        S.op("pe", [sbuf_, *h_b], [bankb[bi]], emit)
                        blk = tile * (TT // 128) + tb
                        S.op("act", [bankb[bi]], [vst_b],
                             lambda E, bi=bi, blk=blk, rv=rv, ncol=ncol: E.activation(
                                 out=vst[:, blk, rv * 256:rv * 256 + ncol], in_=banks[bi][:, 0:ncol], func=AF.Copy))
                pump_mods(2)
            qk_flush()
            kod_b, vod_b, kid_b, vid_b = Buf("kod"), Buf("vod"), Buf("kid"), Buf("vid")
            S.dma("sp", s_kv[0], k_out.ap().rearrange("(k d) t -> d k t", d=128), kst, [kst_b], [kod_b])
            S.dma("sp", s_kv[1], v_out.ap().rearrange("(b p) e -> p b e", p=128), vst, [vst_b], [vod_b])
            for (o_, i_, ob, ib, sc_) in ((k_out, k_in, kod_b, kid_b, s_cc[1]), (v_out, v_in, vod_b, vid_b, s_cc[2])):
                S.custom("pool", sc_, 1, [ob], [ib],
                         lambda E, o_=o_, i_=i_: E.collective_compute(
                             "AllGather", ALU.bypass, replica_groups=[[0, 1], [2, 3], [4, 5], [6, 7]],
                             ins=[o_.ap().opt()], outs=[i_.ap().opt()]))
            names = [f"Q{hq}" for hq in range(NH)] + ["kvg0", "kvg1"]
            rb = r_claim(names)
            Q_b, kvg_b = rb[:NH], rb[NH:]
            Q_t = R_t[:, 0:NH * TT].rearrange("p (h t) -> p h t", h=NH)
            o_kv = NH * TT
            kvsz = 2 * NT + 2 * NB * 128
            s_kvg = [S.sem("d_kvg0"), S.sem("d_kvg1")]
            kin4 = k_in.ap().rearrange("(s k d) t -> k d s t", s=2, d=128)
            vin3 = v_in.ap().rearrange("(b p) e -> p b e", p=128)
            kvn = [0]
            for tile in range(NTILE):
                norm_h(ls, tile)
                load_tables(tile)
                for rq_ in range(cfg.NQC):
                    slot, sbuf_, ssem = ringA.next()
                    S.dma("pool", ssem, slot, wq_d[rq_], [], [sbuf_], max_dma_last_dim=8192)
                    s3 = slot.rearrange("p (k c) -> p k c", k=KC)
                    for hd in range(min(2, NH - rq_ * 2)):
                        hq = rq_ * 2 + hd
                        for tg in range(NTG):
                            bi = pj[0] % 2
                            pj[0] += 1
                            proj_fm(s3, hd * 128, tg, bi, sbuf_)
                            qk_post(bi, 0, Q_t[:, hq, tg * TG:(tg + 1) * TG], Q_b[hq], slice(tg * TG, (tg + 1) * TG))
                    pump_mods(1)
                qk_flush()
                for kh in range(NKV):
                    kk = kvn[0] % 2
                    kvn[0] += 1
                    base = o_kv + kk * kvsz
                    Kg = R_t[:, base:base + 2 * NT].rearrange("p (s t) -> p s t", s=2)
                    Vg = R_t[:, base + 2 * NT:base + kvsz].rearrange("p (b e) -> p b e", e=128)
                    S.dma("sp", s_kvg[kk], Kg, kin4[kh], [kid_b], [kvg_b[kk]])
                    for b0 in range(0, 2 * NB, 8):
                        S.dma("sp", s_kvg[kk], Vg[:, b0:b0 + 8, :], vin3[:, b0:b0 + 8, kh * 128:(kh + 1) * 128], [vid_b],
                              [kvg_b[kk]])
                    for hq in range(kh * GQ, (kh + 1) * GQ):
                        for qg in range(NTG):
                            qsl = slice(qg * TG, (qg + 1) * TG)
                            bO, bZ = 4 + (hq * NTG + qg) % 2, 6 + (hq * NTG + qg) % 2
                            NKB = 2 * NB

                            NP = NKB // 2

                            def s_pair(j, Kg=Kg, hq=hq, qsl=qsl, kk=kk):
                                pb_ = (j % 2) * 2
                                rhs = Q_t[:, hq, qsl]

                                def emit(E):
                                    ins = None
                                    for t_ in range(2):
                                        kb = 2 * j + t_
                                        sl_, kbl = divmod(kb, NB)
                                        ins = E.matmul(psum_all[:, pb_ * 512 + t_ * TG:pb_ * 512 + (t_ + 1) * TG],
                                                       lhsT=Kg[:, sl_, kbl * 128:(kbl + 1) * 128], rhs=rhs, start=True, stop=True)
                                    return ins
                                S.op("pe", [kvg_b[kk], Q_b[hq]], [bankb[pb_], bankb[pb_ + 1]], emit)
                            s_pair(0)
                            if NP > 1:
                                s_pair(1)
                            for j in range(NP):
                                pb_ = (j % 2) * 2
                                sl_ = (2 * j) // NB
                                ke = e_n[0] % NE
                                e_n[0] += 1
                                ea, eb = e_t[:, ke, :], e_b[ke]
                                S.op("act", [bankb[pb_], bankb[pb_ + 1], consts_b], [eb],
                                     lambda E, pb_=pb_, ea=ea, sl_=sl_: E.activation(
                                         out=ea, in_=psum_all[:, pb_ * 512:pb_ * 512 + 2 * TG], func=AF.Exp,
                                         bias=masks_t[:, 2 + sl_:3 + sl_], scale=SM_SCALE))
                                if j + 2 < NP:
                                    s_pair(j + 2)

                                def emit(E, j=j, ea=ea, bO=bO, Vg=Vg, NKB=NKB):
                                    ins = None
                                    for t_ in range(2):
                                        kb = 2 * j + t_
                                        et = ea[:, t_ * TG:(t_ + 1) * TG]
                                        ins = E.matmul(banks[bO][:, 0:TG], lhsT=Vg[:, kb, :], rhs=et, start=(kb == 0),
                                                       stop=(kb == NKB - 1))
                                    return ins
                                S.op("pe", [eb, kvg_b[kk]], [bankb[bO]], emit)
                                e0, e1 = ea[:, 0:TG], ea[:, TG:2 * TG]
                                kq = zt_n[0] % 2
                                zt_n[0] += 1
                                zsa, zsb = zs_t[:, kq, :], zs_b[kq]
                                S.op("dve", [eb], [zsb],
                                     lambda E, zsa=zsa, e0=e0, e1=e1: E.tensor_tensor(out=zsa, in0=e0, in1=e1, op=ALU.add))
                                while zq_pend:
                                    zq_pend.pop(0)()
                                zq_pend.append(lambda j=j, zsa=zsa, zsb=zsb, bZ=bZ, NP=NP: S.op(
                                    "pe", [zsb, consts_b], [bankb[bZ]],
                                    lambda E: E.matmul(banks[bZ][:, 0:TG], lhsT=ones1_t[:], rhs=zsa, start=(j == 0),
                                                       stop=(j == NP - 1))))
                            while zq_pend:
                                zq_pend.pop(0)()
                            kz = rz_n[0] % 2
                            rz_n[0] += 1
                            rza, rzb = rz_t[:, kz, :], rz_b[kz]
                            S.op("dve", [bankb[bZ]], [rzb],
                                 lambda E, rza=rza, bZ=bZ: E.reciprocal(out=rza, in_=banks[bZ][:, 0:TG]))
                            S.op("dve", [bankb[bO], rzb], [h_b[hq]],
                                 lambda E, rza=rza, hq=hq, qsl=qsl, bO=bO: E.tensor_tensor(
                                     out=h_t[:, hq, qsl], in0=banks[bO][:, 0:TG], in1=rza, op=ALU.mult))
                for fc in range(KC):
                    slot, sbuf_, ssem = ringB.next()
                    S.dma("pool", ssem, slot[:, 0:NH * 128], wo_d[fc], [], [sbuf_], max_dma_last_dim=8192)
                    s3 = slot[:, 0:NH * 128].rearrange("p (h c) -> p h c", h=NH)
                    for tg in range(NTG):
                        gf = tile * NTG + tg
                        xs, xsb = load_x(fc, gf)
                        bi = (fc * NTG + tg) % 2

                        def emit(E, bi=bi, s3=s3, tg=tg):
                            ins = None
                            for hq in range(NH):
                                ins = E.matmul(banks[bi][:, 0:TG], lhsT=s3[:, hq, :], rhs=h_t[:, hq, tg * TG:(tg + 1) * TG],
                                               start=(hq == 0), stop=(hq == NH - 1))
                            return ins
                        S.op("pe", [sbuf_, *h_b], [bankb[bi]], emit)
                        st_ = (6 + tg, fc == 0, fc == KC - 1) if ls + 1 < NSUB else None
                        store_x(fc, gf, bi, modG_t[:, ls, fc:fc + 1], xs, xsb, [modsG_b[ls]], stat=st_)
                if ls + 1 < NSUB:
                    stat_fin([(6 + tg, tile * NTG + tg) for tg in range(NTG)])

        schedule_mods(0)
        pump_mods(2 * RSUB // 3 + 1)
        mods_sched = [1]

        def is_ffn(l_):
            return l_ % 3 != 1
        for i in range(L):
            for s_ in range(3):
                ls = i * 3 + s_
                if is_ffn(ls):
                    nf = next((l_ for l_ in range(ls + 1, NSUB) if is_ffn(l_)), NSUB - 1)
                    while mods_sched[0] <= nf and mods_sched[0] < NSUB:
                        schedule_mods(mods_sched[0])
                        mods_sched[0] += 1
                if s_ == 0:
                    ffn(ls, i, 0)
                elif s_ == 2:
                    ffn(ls, i, 1)
                elif i % 2 == 0:
                    pool_mixer(ls, i // 2)
                else:
                    attention(ls, i // 2)
                pump_mods()
        S.wait_all("sp", [b for row in y_b for b in row])

        _LAST_SCHED[0] = S
        block = es.enter_context(nc.Block())

        @block.tensor
        def _(E):
            for f in S.q["pe"]:
                f(E)

        @block.scalar
        def _(E):
            for f in S.q["act"]:
                f(E)

        @block.vector
        def _(E):
            for f in S.q["dve"]:
                f(E)

        @block.gpsimd
        def _(E):
            for f in S.q["pool"]:
                f(E)

        @block.sync
        def _(E):
            for f in S.q["sp"]:
                f(E)
    return nc


def _ring_tiles(w, KC):
    K, ncols = w.shape
    r = ncols // 256
    return np.ascontiguousarray(w.reshape(KC, 128, r, 256).transpose(2, 1, 0, 3)).reshape(r, 128, KC * 256)


def _pad_cols(w, mult=256):
    K, n = w.shape
    if n % mult == 0:
        return w
    out = np.zeros((K, (n + mult - 1) // mult * mult), w.dtype)
    out[:, :n] = w
    return out


def _rope_tables(cfg, t0):
    t = np.arange(t0, t0 + cfg.NT)
    r = (t // cfg.GRID_W).astype(np.float32)
    c = (t % cfg.GRID_W).astype(np.float32)
    inv = (ROPE_THETA ** (-np.arange(0, 64, 2, dtype=np.float32) / np.float32(64))).astype(np.float32)
    ang_r = (r[:, None] * inv[None, :]).astype(np.float32)
    ang_c = (c[:, None] * inv[None, :]).astype(np.float32)
    cosT = np.empty((128, cfg.NT), np.float32)
    sinS = np.empty((128, cfg.NT), np.float32)
    for base, ang in ((0, ang_r), (64, ang_c)):
        cs, sn = np.cos(ang).T.astype(np.float32), np.sin(ang).T.astype(np.float32)
        cosT[base:base + 32] = cs
        cosT[base + 32:base + 64] = cs
        sinS[base:base + 32] = -sn
        sinS[base + 32:base + 64] = sn
    return cosT, sinS


def _perm_matrix():
    pm = np.zeros((128, 128), np.float32)
    for m in range(128):
        k = m + 32 if (m % 64) < 32 else m - 32
        pm[k, m] = 1.0
    return pm


def _invc(cfg, t0, S):
    out = np.zeros((4, 16), np.float32)
    loc = list(range(8)) + list(range(cfg.NT - 8, cfg.NT))
    for gi, w in enumerate(POOL_WINDOWS):
        for n, tl in enumerate(loc):
            t = t0 + tl
            lo, hi = max(t - w // 2, 0), min(t + w // 2, S)
            out[gi, n] = 1.0 / float(hi - lo)
    return np.ascontiguousarray(np.broadcast_to(out.reshape(1, 64), (128, 64)))


def host_prepare(cfg, x_prompt, x_sample, c_prompt, c_sample, ada_w, ada_b, norm_g, ffn_w_in, ffn_w_out,
                 pool_w, pool_scale, attn_w_qkv, attn_q_g, attn_k_g, attn_w_o):
    KC, D, NT, NJ, NJH, NH, NKV, L = cfg.KC, cfg.D, cfg.NT, cfg.NJ, cfg.NJH, cfg.NH, cfg.NKV, cfg.L
    f = lambda a: np.asarray(a, dtype=np.float32)
    x_prompt, x_sample, c_prompt, c_sample = f(x_prompt), f(x_sample), f(c_prompt), f(c_sample)
    ada_w, ada_b, norm_g, ffn_w_in, ffn_w_out = f(ada_w), f(ada_b), f(norm_g), f(ffn_w_in), f(ffn_w_out)
    pool_w, pool_scale, attn_w_qkv, attn_q_g, attn_k_g, attn_w_o = (f(pool_w), f(pool_scale), f(attn_w_qkv),
                                                                    f(attn_q_g), f(attn_k_g), f(attn_w_o))
    NSUB = 3 * L
    sh = {}
    sh["ada_w"] = np.stack([_ring_tiles(ada_w[i], KC) for i in range(L)])
    sh["ada_b"] = np.ascontiguousarray(ada_b.reshape(L * 3 * 3 * KC, 128).T)
    sh["norm_g"] = np.ascontiguousarray(norm_g.reshape(NSUB * KC, 128).T)
    win = np.empty((L, 2, NJ, 128, KC * 256), np.float32)
    wout = np.empty((L, 2, 2, KC, 128, NJH * 128), np.float32)
    for i in range(L):
        for w_ in range(2):
            wi = ffn_w_in[i, w_]
            g = wi[:, :cfg.DFF].reshape(KC, 128, NJ, 128)
            u = wi[:, cfg.DFF:].reshape(KC, 128, NJ, 128)
            gu = np.concatenate([g, u], axis=3)
            win[i, w_] = gu.transpose(2, 1, 0, 3).reshape(NJ, 128, KC * 256)
            wo = ffn_w_out[i, w_].reshape(2, NJH, 128, KC, 128)
            wout[i, w_] = wo.transpose(0, 3, 2, 1, 4).reshape(2, KC, 128, NJH * 128)
    sh["w_in"], sh["w_out"] = win, wout
    pw = pool_w[0].reshape(4, cfg.ICG, 128, cfg.GC)
    sh["pool_w"] = np.ascontiguousarray(pw.transpose(0, 2, 1, 3)).reshape(4, 128, cfg.ICG * cfg.GC)
    sh["pool_scale"] = np.ascontiguousarray(pool_scale[0].reshape(KC, 128).T)
    wqkv = attn_w_qkv[0]
    sh["w_q"] = _ring_tiles(_pad_cols(wqkv[:, :NH * 128]), KC)
    sh["w_k"] = _ring_tiles(_pad_cols(wqkv[:, NH * 128:(NH + NKV) * 128]), KC)
    sh["w_v"] = _ring_tiles(_pad_cols(wqkv[:, (NH + NKV) * 128:]), KC)
    sh["qkg"] = np.ascontiguousarray(np.stack([attn_q_g[0], attn_k_g[0]], axis=1))
    wo = attn_w_o[0].reshape(NH, 128, KC, 128)
    sh["w_o"] = np.ascontiguousarray(wo.transpose(2, 1, 0, 3)).reshape(KC, 128, NH * 128)
    sh["PT"] = _perm_matrix()
    in_maps = []
    for core in range(8):
        if core < 4:
            b, half = divmod(core, 2)
            xs = x_prompt[b, half * NT:(half + 1) * NT]
            cv = c_prompt[b]
            t0, Sq = half * NT, 2 * NT
            lm, rm = (0.0, 1.0) if half == 0 else (1.0, 0.0)
            kb0, kb1 = 0.0, 0.0
        else:
            b = core - 4
            xs = x_sample[b]
            cv = c_sample[b]
            t0, Sq = 0, NT
            lm, rm = 0.0, 0.0
            kb0, kb1 = (0.0, -30000.0) if core % 2 == 0 else (-30000.0, 0.0)
        m = dict(sh)
        m["xT"] = np.ascontiguousarray(xs.T)
        m["cvec"] = np.ascontiguousarray(cv.reshape(KC, 128).T)
        m["cosT"], m["sinS"] = _rope_tables(cfg, t0)
        m["invc"] = _invc(cfg, t0, Sq)
        m["masks"] = np.ascontiguousarray(np.broadcast_to(np.array([[lm, rm, kb0, kb1]], np.float32), (128, 4)))
        in_maps.append(m)
    return in_maps


_PROG_CACHE = {}


def run(cfg, **inputs):
    in_maps = host_prepare(cfg, **inputs)
    key = id(cfg)
    if key not in _PROG_CACHE:
        _PROG_CACHE[key] = build_program(cfg)
    nc = _PROG_CACHE[key]
    res = run_bass_kernel_spmd(nc, in_maps, core_ids=list(range(8)))
    NT, D = cfg.NT, cfg.D
    yp = np.empty((2, 2 * NT, D), np.float32)
    ys = np.empty((4, NT, D), np.float32)
    for core in range(8):
        y = np.asarray(res.results[core]["yT"]).T
        if core < 4:
            b, half = divmod(core, 2)
            yp[b, half * NT:(half + 1) * NT] = y
        else:
            ys[core - 4] = y
    return yp, ys


def kernel(**inputs):
    return run(CFG_FULL, **inputs)
```

```python
import math
from contextlib import ExitStack
import numpy as np
import concourse.bass as bass
import concourse.mybir as mybir
from concourse.bass_utils import run_bass_kernel_spmd

F32 = mybir.dt.float32
BF16 = mybir.dt.bfloat16
AF = mybir.ActivationFunctionType
ALU = mybir.AluOpType
EPS = 1e-6
ROPE_THETA = 10000.0
POOL_WINDOWS = (2, 4, 8, 16)
import os
OPT_NORM_OVERLAP = os.environ.get("K_NORM", "1") == "1"
OPT_QK_PIPE = os.environ.get("K_QK", "1") == "1"
OPT_POOL_GPSIMD = os.environ.get("K_POOL", "0") == "1"


class Cfg:
    def __init__(s, D, DFF, NT, TT, TG, NH, NKV, GRID_W, L=2):
        s.D, s.DFF, s.NT, s.TT, s.TG, s.NH, s.NKV, s.GRID_W, s.L = D, DFF, NT, TT, TG, NH, NKV, GRID_W, L
        s.KC = D // 128
        s.NJ = DFF // 128
        s.NJH = s.NJ // 2
        s.NTILE = NT // TT
        s.NTG = TT // TG
        s.NTGF = NT // TG
        s.GC = D // 4
        s.ICG = s.GC // 128
        s.RSUB = 3 * D // 256
        s.NQC = (NH + 1) // 2
        s.NKC = (NKV + 1) // 2
        s.NVC = (NKV + 1) // 2
        s.NB = NT // 128
        s.SEQ_P = 2 * NT
        s.SEQ_S = NT
        assert NH * 128 == D and s.NJ % 2 == 0 and TG <= 512 and TT % TG == 0


CFG_FULL = Cfg(D=2048, DFF=5632, NT=2048, TT=1024, TG=512, NH=16, NKV=4, GRID_W=64)


_LAST_SCHED = [None]


class Buf:
    __slots__ = ("name", "w", "r")

    def __init__(self, name):
        self.name = name
        self.w = None
        self.r = {}


class Sched:
    ENGS = ("pe", "act", "dve", "pool", "sp")

    def __init__(self, nc, es):
        self.nc, self.es = nc, es
        self.q = {e: [] for e in self.ENGS}
        self.tr = {e: [] for e in self.ENGS}
        self.sems = {}
        self.cnt = {}
        self.waited = {}
        for e in ("pe", "act", "dve", "pool"):
            self.sem("e_" + e)

    def sem(self, name):
        h = self.es.enter_context(self.nc.semaphore(name))
        self.sems[name] = h
        self.cnt[name] = 0
        return name

    def _deps(self, eng, reads, writes, extra=()):
        best = {}

        def add(ev):
            if ev is None:
                return
            k, v = ev
            if best.get(k, 0) < v:
                best[k] = v
        for b in reads:
            add(b.w)
        for b in writes:
            add(b.w)
            for k, v in b.r.items():
                add((k, v))
        for ev in extra:
            add(ev)
        for k, v in best.items():
            if eng == "pe" and k == "e_pe":
                continue
            key = (eng, k)
            if self.waited.get(key, 0) < v:
                self.waited[key] = v
                h = self.sems[k]
                self.tr[eng].append(("wait", k, v))
                self.q[eng].append(lambda E, h=h, v=v: E.wait_ge(h, v))

    def _mark(self, ev, reads, writes):
        for b in writes:
            b.w = ev
            b.r = {}
        for b in reads:
            k, v = ev
            if b.r.get(k, 0) < v:
                b.r[k] = v

    def op(self, eng, reads, writes, emit, extra=()):
        self._deps(eng, reads, writes, extra)
        k = "e_" + eng
        self.cnt[k] += 1
        ev = (k, self.cnt[k])
        h = self.sems[k]
        self.tr[eng].append(("inc", k, 1))
        self.q[eng].append(lambda E, h=h: emit(E).then_inc(h, 1))
        self._mark(ev, reads, writes)
        return ev

    def dma(self, queue, dsem, out_ap, in_ap, reads, writes, extra=(), **kw):
        self._deps(queue, reads, writes, extra)
        self.cnt[dsem] += 16
        ev = (dsem, self.cnt[dsem])
        h = self.sems[dsem]
        self.tr[queue].append(("inc", dsem, 16))
        self.q[queue].append(lambda E, h=h: E.dma_start(out=out_ap, in_=in_ap, **kw).then_inc(h, 16))
        self._mark(ev, reads, writes)
        return ev

    def custom(self, queue, dsem, inc, reads, writes, emit):
        self._deps(queue, reads, writes)
        self.cnt[dsem] += inc
        ev = (dsem, self.cnt[dsem])
        h = self.sems[dsem]
        self.tr[queue].append(("inc", dsem, inc))
        self.q[queue].append(lambda E, h=h: emit(E).then_inc(h, inc))
        self._mark(ev, reads, writes)
        return ev

    def wait_all(self, eng, bufs):
        self._deps(eng, bufs, bufs)


def fence_events(bufs):
    evs = []
    for b in bufs:
        if b.w is not None:
            evs.append(b.w)
        evs.extend(b.r.items())
    return evs


class Ring:
    def __init__(self, S, name, aps):
        self.S = S
        self.aps = aps
        self.bufs = [Buf(f"{name}{i}") for i in range(len(aps))]
        self.sems = [S.sem(f"d_{name}{i}") for i in range(len(aps))]
        self.n = 0

    def next(self):
        i = self.n % len(self.aps)
        self.n += 1
        return self.aps[i], self.bufs[i], self.sems[i]


def build_program(cfg):
    D, KC, NT, TT, TG, NTG, NTILE = cfg.D, cfg.KC, cfg.NT, cfg.TT, cfg.TG, cfg.NTG, cfg.NTILE
    NJ, NJH, NH, NKV, L = cfg.NJ, cfg.NJH, cfg.NH, cfg.NKV, cfg.L
    GQ = NH // NKV
    ICG, GC, NB = cfg.ICG, cfg.GC, cfg.NB
    NTGF = cfg.NTGF
    RSUB = cfg.RSUB
    NSUB = 3 * L
    SM_SCALE = 1.0 / math.sqrt(128.0)

    nc = bass.Bass("TRN2", target_bir_lowering=False)

    def din(name, shape, dt=F32):
        return nc.dram_tensor(name, list(shape), dt, kind="ExternalInput").ap()

    xT = din("xT", [D, NT])
    yT = nc.dram_tensor("yT", [D, NT], F32, kind="ExternalOutput").ap()
    cvec_d = din("cvec", [128, KC])
    adaw_d = din("ada_w", [L, 3 * RSUB, 128, KC * 256])
    adab_d = din("ada_b", [128, NSUB * 3 * KC])
    normg_d = din("norm_g", [128, NSUB * KC])
    win_d = din("w_in", [L, 2, NJ, 128, KC * 256])
    wout_d = din("w_out", [L, 2, 2, KC, 128, NJH * 128])
    poolw_d = din("pool_w", [4, 128, ICG * GC])
    pscale_d = din("pool_scale", [128, KC])
    wq_d = din("w_q", [cfg.NQC, 128, KC * 256])
    wk_d = din("w_k", [cfg.NKC, 128, KC * 256])
    wv_d = din("w_v", [cfg.NVC, 128, KC * 256])
    qkg_d = din("qkg", [128, 2])
    wo_d = din("w_o", [KC, 128, NH * 128])
    cos_d = din("cosT", [128, NT])
    sin_d = din("sinS", [128, NT])
    pt_d = din("PT", [128, 128])
    invc_d = din("invc", [128, 64])
    masks_d = din("masks", [128, 4])

    halo_out = nc.dram_tensor("halo_out", [128, KC * 16], F32)
    halo_in = nc.dram_tensor("halo_in", [2 * 128, KC * 16], F32)
    k_out = nc.dram_tensor("k_out", [NKV * 128, NT], BF16)
    k_in = nc.dram_tensor("k_in", [2 * NKV * 128, NT], BF16)
    v_out = nc.dram_tensor("v_out", [NT, NKV * 128], BF16)
    v_in = nc.dram_tensor("v_in", [2 * NT, NKV * 128], BF16)

    xT3 = xT.rearrange("(c p) t -> c p t", p=128)
    yT3 = yT.rearrange("(c p) t -> c p t", p=128)

    es = ExitStack()
    with es:
        def sb(name, shape, dt):
            return es.enter_context(nc.sbuf_tensor("sb_" + name, list(shape), dt))

        S = Sched(nc, es)

        h_t = sb("h", [128, KC, TT], BF16)
        r_ffn = NJH * TT
        LP = NT + 16
        r_pool = 8 * LP + 2 * ICG * NT
        r_attn = max(NKV * NT + NB * NKV * 128, NH * TT + 2 * (2 * NT + 2 * NB * 128))
        R_EL = max(r_ffn, r_pool, r_attn)
        R_t = sb("R", [128, R_EL], BF16)
        WA_EL = KC * 256
        NSA = 3
        ringA_t = sb("ringA", [128, NSA, WA_EL], BF16)
        WB_EL = max(NJH, NH) * 128
        NSB = 2
        ringB_t = sb("ringB", [128, NSB, WB_EL], BF16)
        NXS = 4
        xst_t = sb("xstage", [128, NXS, TG], F32)
        NXO = 3
        xout_t = sb("xout", [128, NXO, TG], F32)
        rstd_t = sb("rstd", [128, NT], F32)
        NSQ = 3
        sq_t = sb("sq", [128, NSQ, TG], BF16)
        NSG = 2
        NE = 3
        e_t = sb("ebuf", [128, NE, 2 * TG], BF16)
        tabc_t = sb("tabc", [128, TT], F32)
        tabs_t = sb("tabs", [128, TT], F32)
        NQT = 2
        u_t = sb("qu", [128, NQT, TG], F32)
        sg_t = u_t
        t1_t = sb("qt1", [128, NQT, TG], F32)
        t2_t = sb("qt2", [128, NQT, TG], F32)
        rq_t = sb("qrq", [128, NQT, TG], F32)
        rz_t = sb("rz", [128, 2, TG], F32)
        zt_t = sb("zt", [128, 2, TG], F32)
        zs_t = sb("zs", [128, 2, TG], BF16)
        epsc_t = sb("epsc", [128, 1], F32)
        onesD_t = sb("onesD", [128, 128], BF16)
        onesH_t = sb("onesH", [128, 128], BF16)
        ones1_t = sb("ones1", [128, 128], BF16)
        pt_t = sb("pt", [128, 128], F32)
        cvec_t = sb("cvec", [128, KC], F32)
        sc_t = sb("sc", [128, KC], BF16)
        adab_t = sb("adab", [128, NSUB * 3 * KC], F32)
        normg_t = sb("normg", [128, NSUB * KC], F32)
        pscale_t = sb("pscale", [128, KC], F32)
        qkg_t = sb("qkg", [128, 2], F32)
        invc_t = sb("invc", [128, 64], F32)
        masks_t = sb("masks", [128, 4], F32)
        modraw_t = sb("modraw", [128, 3 * KC], F32)
        modA_t = sb("modA", [128, NSUB, KC], F32)
        modB_t = sb("modB", [128, NSUB, KC], F32)
        modG_t = sb("modG", [128, NSUB, KC], F32)
        xb_t = sb("xb", [128, KC, 16], F32)
        re_t = sb("re", [128, 16], F32)
        hal_t = sb("hal", [128, 2, KC, 16], F32)
        halL_t = sb("halL", [128, KC, 8], F32)
        halR_t = sb("halR", [128, KC, 8], F32)
        e16_t = sb("e16", [128, 2, 8], F32)

        psum_all = es.enter_context(nc.psum_tensor("psum_all", [128, 8 * 512], F32))
        banks = [psum_all[:, i * 512:(i + 1) * 512] for i in range(8)]
        bankb = [Buf(f"bank{i}") for i in range(8)]

        h_b = [Buf(f"h{c}") for c in range(KC)]
        rstd_b = [Buf(f"rstd{t}") for t in range(NTILE)]
        consts_b = Buf("consts")
        sc_b = Buf("sc")
        mods_b = [Buf(f"mods{i}") for i in range(NSUB)]
        modsG_b = [Buf(f"modsG{i}") for i in range(NSUB)]
        modraw_b = Buf("modraw")
        tab_b = Buf("tab")
        y_b = [[Buf(f"y{c}_{g}") for g in range(NTGF)] for c in range(KC)]

        ringA = Ring(S, "A", [ringA_t[:, i, :] for i in range(NSA)])
        ringB = Ring(S, "B", [ringB_t[:, i, :] for i in range(NSB)])
        xst = Ring(S, "xs", [xst_t[:, i, :] for i in range(NXS)])
        xout_aps = [xout_t[:, i, :] for i in range(NXO)]
        xout_b = [Buf(f"xo{i}") for i in range(NXO)]
        xout_s = [S.sem(f"d_xo{i}") for i in range(NXO)]
        xout_n = [0]
        sq_b = [Buf(f"sq{i}") for i in range(NSQ)]
        sq_n = [0]
        sg_b = None
        sg_n = [0]
        e_b = [Buf(f"e{i}") for i in range(NE)]
        e_n = [0]
        qt_b = [[Buf(f"q{n}{i}") for i in range(NQT)] for n in ("u", "t1", "t2", "rq")]
        qt_n = [0]
        sg_b = qt_b[0][:NSG]
        rz_b = [Buf("rz0"), Buf("rz1")]
        rz_n = [0]
        zt_b = [Buf("zt0"), Buf("zt1")]
        zs_b = [Buf("zs0"), Buf("zs1")]
        zt_n = [0]
        zq_pend = []

        s_init = S.sem("d_init")
        s_misc = S.sem("d_misc")
        s_cc = [S.sem(f"d_cc{i}") for i in range(3)]
        s_tab = S.sem("d_tab")
        s_hh1 = S.sem("d_hh1")
        s_kv = [S.sem("d_kv0"), S.sem("d_kv1")]

        R_last = [[]]

        def r_claim(names):
            evs = fence_events(R_last[0])
            bufs = []
            for n in names:
                b = Buf(n)
                for k, v in evs:
                    if b.r.get(k, 0) < v:
                        b.r[k] = v
                bufs.append(b)
            R_last[0] = bufs
            return bufs

        init_loads = [(cvec_t[:], cvec_d), (adab_t[:], adab_d), (normg_t[:], normg_d), (pscale_t[:], pscale_d),
                      (qkg_t[:], qkg_d), (invc_t[:], invc_d), (masks_t[:], masks_d), (pt_t[:], pt_d)]
        for o, i_ in init_loads:
            ev = S.dma("sp", s_init, o, i_, [], [consts_b])
        S.op("dve", [], [consts_b], lambda E: (E.memset(onesD_t[:], 1.0 / D), E.memset(onesH_t[:], 1.0 / 128.0),
                                                E.memset(epsc_t[:], EPS), E.memset(ones1_t[:], 1.0))[-1], extra=[ev])
        consts_b.r = {}
        S.op("act", [consts_b], [sc_b], lambda E: E.activation(out=sc_t[:], in_=cvec_t[:], func=AF.Silu))

        pending_mods = []

        def schedule_mods(ls):
            i, s_ = divmod(ls, 3)
            bk, bkb = banks[7], bankb[7]

            def chunk(r):
                def run():
                    slot, sbuf_, ssem = ringA.next()
                    S.dma("pool", ssem, slot, adaw_d[i, s_ * RSUB + r], [], [sbuf_], max_dma_last_dim=8192)
                    s3 = slot.rearrange("p (k c) -> p k c", k=KC)

                    def emit(E):
                        ins = None
                        for hf in range(2):
                            oc = r * 2 + hf
                            for kc in range(KC):
                                ins = E.matmul(bk[:, oc:oc + 1], lhsT=s3[:, kc, hf * 128:(hf + 1) * 128],
                                               rhs=sc_t[:, kc:kc + 1], start=(kc == 0), stop=(kc == KC - 1))
                        return ins
                    S.op("pe", [sbuf_, sc_b], [bkb], emit)
                    base = ls * 3 * KC + 2 * r
                    S.op("dve", [bkb, consts_b], [modraw_b],
                         lambda E: E.tensor_tensor(out=modraw_t[:, 2 * r:2 * r + 2], in0=bk[:, 2 * r:2 * r + 2],
                                                   in1=adab_t[:, base:base + 2], op=ALU.add))
                return run

            def fin_ab():
                def emit(E):
                    E.tensor_copy(out=modB_t[:, ls, :], in_=modraw_t[:, 0:KC])
                    return E.scalar_tensor_tensor(out=modA_t[:, ls, :], in0=modraw_t[:, KC:2 * KC], scalar=1.0,
                                                  in1=normg_t[:, ls * KC:(ls + 1) * KC], op0=ALU.add, op1=ALU.mult)
                S.op("dve", [modraw_b, consts_b], [mods_b[ls]], emit)

            def fin_g():
                kind = ls % 3

                def emit(E):
                    if kind == 1 and (ls // 3) % 2 == 0:
                        return E.tensor_tensor(out=modG_t[:, ls, :], in0=modraw_t[:, 2 * KC:3 * KC], in1=pscale_t[:],
                                               op=ALU.mult)
                    coef = 1.0 if kind == 1 else 0.5
                    return E.tensor_scalar(out=modG_t[:, ls, :], in0=modraw_t[:, 2 * KC:3 * KC], scalar1=coef,
                                           scalar2=None, op0=ALU.mult)
                S.op("dve", [modraw_b, consts_b], [modsG_b[ls]], emit)
            nab = 2 * RSUB // 3
            for r in range(nab):
                pending_mods.append(chunk(r))
            pending_mods.append(fin_ab)
            for r in range(nab, RSUB):
                pending_mods.append(chunk(r))
            pending_mods.append(fin_g)

        def pump_mods(n=None):
            k = 0
            while pending_mods and (n is None or k < n):
                pending_mods.pop(0)()
                k += 1

        written = [[False] * NTGF for _ in range(KC)]

        def x_src(c, gf):
            return (yT3 if written[c][gf] else xT3)[c][:, gf * TG:(gf + 1) * TG]

        def load_x(c, gf):
            slot, sbuf_, ssem = xst.next()
            S.dma("sp", ssem, slot, x_src(c, gf), [y_b[c][gf]], [sbuf_])
            return slot, sbuf_

        stat_pend = []

        def stat_flush():
            while stat_pend:
                stat_pend.pop(0)()

        def store_x(c, gf, bank_i, G_ap, xold, xold_b, extra_reads=(), stat=None):
            k = xout_n[0] % NXO
            xout_n[0] += 1
            xo, xob = xout_aps[k], xout_b[k]
            S.op("dve", [bankb[bank_i], xold_b, *extra_reads], [xob],
                 lambda E: E.scalar_tensor_tensor(out=xo, in0=banks[bank_i][:, 0:TG], scalar=G_ap, in1=xold,
                                                  op0=ALU.mult, op1=ALU.add))
            S.dma("act", xout_s[k], yT3[c][:, gf * TG:(gf + 1) * TG], xo, [xob], [y_b[c][gf]])
            written[c][gf] = True
            if stat is not None:
                sbk, first, last = stat
                kq = sq_n[0] % NSQ
                sq_n[0] += 1
                sqa, sqb = sq_t[:, kq, :], sq_b[kq]
                S.op("act", [xob], [sqb], lambda E: E.activation(out=sqa, in_=xo, func=AF.Square))
                stat_flush()
                stat_pend.append(lambda: S.op(
                    "pe", [sqb, consts_b], [bankb[sbk]],
                    lambda E: E.matmul(banks[sbk][:, 0:TG], lhsT=onesD_t[:], rhs=sqa, start=first, stop=last)))

        def stat_fin(pairs):
            stat_flush()
            for sbk, gf in pairs:
                dst = rstd_t[:, gf * TG:(gf + 1) * TG]
                rb = rstd_b[gf // NTG]
                S.op("act", [bankb[sbk], consts_b], [rb],
                     lambda E, sbk=sbk, dst=dst: E.activation(out=dst, in_=banks[sbk][:, 0:TG], func=AF.Ln,
                                                              bias=epsc_t[:, 0:1], scale=1.0))
                S.op("act", [rb], [rb], lambda E, dst=dst: E.activation(out=dst, in_=dst, func=AF.Exp, bias=0.0, scale=-0.5))

        def norm_stats_items(tile, rstd_ap, rstd_buf, bank_ids):
            items = []
            pend = []
            for c in range(KC):
                for tg in range(NTG):
                    def item(c=c, tg=tg):
                        gf = tile * NTG + tg
                        xs, xsb = load_x(c, gf)
                        k = sq_n[0] % NSQ
                        sq_n[0] += 1
                        sqa, sqb = sq_t[:, k, :], sq_b[k]
                        S.op("act", [xsb], [sqb], lambda E: E.activation(out=sqa, in_=xs, func=AF.Square))
                        bi = bank_ids[tg]
                        while pend:
                            pend.pop(0)()
                        pend.append(lambda: S.op(
                            "pe", [sqb, consts_b], [bankb[bi]],
                            lambda E: E.matmul(banks[bi][:, 0:TG], lhsT=onesD_t[:], rhs=sqa,
                                               start=(c == 0), stop=(c == KC - 1))))
                    items.append(item)

            def fin():
                while pend:
                    pend.pop(0)()
                for tg in range(NTG):
                    bi = bank_ids[tg]
                    dst = rstd_ap[:, tg * TG:(tg + 1) * TG]
                    S.op("act", [bankb[bi]], [rstd_buf],
                         lambda E, bi=bi, dst=dst: E.activation(out=dst, in_=banks[bi][:, 0:TG], func=AF.Sqrt, bias=EPS,
                                                                scale=1.0))
                    S.op("dve", [rstd_buf], [rstd_buf], lambda E, dst=dst: E.reciprocal(out=dst, in_=dst))
            return items, fin

        def norm_stats(tile, rstd_ap, rstd_buf, bank_ids):
            items, fin = norm_stats_items(tile, rstd_ap, rstd_buf, bank_ids)
            for it in items:
                it()
            fin()

        def norm_h_items(ls, tile):
            items = []
            for c in range(KC):
                for tg in range(NTG):
                    def item(c=c, tg=tg):
                        gf = tile * NTG + tg
                        xs, xsb = load_x(c, gf)
                        S.op("dve", [xsb, rstd_b[tile]], [xsb],
                             lambda E: E.tensor_tensor(out=xs, in0=xs, in1=rstd_t[:, gf * TG:(gf + 1) * TG], op=ALU.mult))
                        S.op("act", [xsb, mods_b[ls]], [h_b[c]],
                             lambda E: E.activation(out=h_t[:, c, tg * TG:(tg + 1) * TG], in_=xs, func=AF.Identity,
                                                    bias=modB_t[:, ls, c:c + 1], scale=modA_t[:, ls, c:c + 1]))
                    items.append(item)
            return items

        prenormed = [None]
        stats0_done = [False]

        def norm_h(ls, tile):
            if prenormed[0] == (ls, tile):
                prenormed[0] = None
                return
            if ls == 0 and not (tile == 0 and stats0_done[0]):
                norm_stats(tile, rstd_t[:, tile * TT:(tile + 1) * TT], rstd_b[tile], [6, 7])
            for it in norm_h_items(ls, tile):
                it()

        def next_unit(ls, tile):
            if tile + 1 < NTILE:
                return (ls, tile + 1)
            nls = ls + 1
            if nls >= NSUB:
                return None
            kind, layer = nls % 3, nls // 3
            if kind in (0, 2) or layer % 2 == 1:
                return (nls, 0)
            return None

        def ffn(ls, i, which):
            a_bufs = r_claim([f"a{j}_{g}" for j in range(NJH) for g in range(NTG)])
            a_t = R_t[:, 0:NJH * TT].rearrange("p (j t) -> p j t", j=NJH)

            def ab(jj, tg):
                return a_bufs[jj * NTG + tg]
            for tile in range(NTILE):
                norm_h(ls, tile)
                nxt = next_unit(ls, tile) if OPT_NORM_OVERLAP else None
                if nxt is not None and nxt[0] != ls:
                    pump_mods()
                for hh in range(2):
                    side, side_fin = [], None
                    if nxt is not None:
                        if hh == 0:
                            if ls == 0 and nxt[0] == 0:
                                side, side_fin = norm_stats_items(nxt[1], rstd_t[:, nxt[1] * TT:(nxt[1] + 1) * TT],
                                                                  rstd_b[nxt[1]], [6, 7])
                        else:
                            side = norm_h_items(nxt[0], nxt[1])
                    for jj in range(NJH):
                        j = hh * NJH + jj
                        slot, sbuf_, ssem = ringA.next()
                        S.dma("pool", ssem, slot, win_d[i, which, j], [], [sbuf_], max_dma_last_dim=8192)
                        s3 = slot.rearrange("p (k c) -> p k c", k=KC)
                        for tg in range(NTG):
                            bg, bu = tg % 2, 2 + tg % 2
                            for (bi, off) in ((bg, 0), (bu, 128)):
                                def emit(E, bi=bi, off=off, tg=tg, s3=s3):
                                    ins = None
                                    for kc in range(KC):
                                        ins = E.matmul(banks[bi][:, 0:TG], lhsT=s3[:, kc, off:off + 128],
                                                       rhs=h_t[:, kc, tg * TG:(tg + 1) * TG], start=(kc == 0),
                                                       stop=(kc == KC - 1))
                                    return ins
                                S.op("pe", [sbuf_, *h_b], [bankb[bi]], emit)
                            k = sg_n[0] % NSG
                            sg_n[0] += 1
                            sga, sgb = sg_t[:, k, :], sg_b[k]
                            S.op("act", [bankb[bg]], [sgb],
                                 lambda E, bg=bg, sga=sga: E.activation(out=sga, in_=banks[bg][:, 0:TG], func=AF.Silu))
                            S.op("dve", [bankb[bu], sgb], [ab(jj, tg)],
                                 lambda E, bu=bu, sga=sga, jj=jj, tg=tg: E.tensor_tensor(
                                     out=a_t[:, jj, tg * TG:(tg + 1) * TG], in0=banks[bu][:, 0:TG], in1=sga, op=ALU.mult))
                        pump_mods(1)
                    for fc in range(KC):
                        slot, sbuf_, ssem = ringB.next()
                        S.dma("pool", ssem, slot[:, 0:NJH * 128], wout_d[i, which, hh, fc], [], [sbuf_],
                              max_dma_last_dim=8192)
                        s3 = slot[:, 0:NJH * 128].rearrange("p (j c) -> p j c", j=NJH)
                        for tg in range(NTG):
                            gf = tile * NTG + tg
                            xs, xsb = load_x(fc, gf)
                            bi = 4 + (fc * NTG + tg) % 2

                            def emit(E, bi=bi, s3=s3, tg=tg):
                                ins = None
                                for jj in range(NJH):
                                    ins = E.matmul(banks[bi][:, 0:TG], lhsT=s3[:, jj, :],
                                                   rhs=a_t[:, jj, tg * TG:(tg + 1) * TG], start=(jj == 0),
                                                   stop=(jj == NJH - 1))
                                return ins
                            S.op("pe", [sbuf_] + [ab(jj, tg) for jj in range(NJH)], [bankb[bi]], emit)
                            st_ = (6 + tg, fc == 0, fc == KC - 1) if (hh == 1 and ls + 1 < NSUB) else None
                            store_x(fc, gf, bi, modG_t[:, ls, fc:fc + 1], xs, xsb, [modsG_b[ls]], stat=st_)
                            if side:
                                side.pop(0)()
                    while side:
                        side.pop(0)()
                    if side_fin is not None:
                        side_fin()
                    if hh == 1 and ls + 1 < NSUB:
                        stat_fin([(6 + tg, tile * NTG + tg) for tg in range(NTG)])
                if nxt is not None:
                    prenormed[0] = nxt

        def pool_mixer(ls, j_):
            names = ["hh0", "hh1", "sA", "sB"] + [f"pb{k}_{ic}" for k in range(2) for ic in range(ICG)]
            hh0_b, hh1_b, sA_b, sB_b, *pb_all = r_claim(names)
            Rf = R_t[:, 0:8 * LP].bitcast(F32)
            rstdP = rstd_t[:, 0:NT]
            hhs = [(Rf[:, 0:LP], hh0_b), (Rf[:, LP:2 * LP], hh1_b)]
            sA = Rf[:, 2 * LP:3 * LP]
            sB = Rf[:, 3 * LP:4 * LP]
            o_pb = 8 * LP
            pbs = [R_t[:, o_pb + k * ICG * NT:o_pb + (k + 1) * ICG * NT].rearrange("p (i t) -> p i t", i=ICG)
                   for k in range(2)]
            pbbs = [pb_all[0:ICG], pb_all[ICG:2 * ICG]]
            A_ = modA_t[:, ls, :]
            B_ = modB_t[:, ls, :]
            s_hh = [s_tab, s_hh1]
            fronted = set()

            def chunk_front(c):
                if c in fronted:
                    return
                fronted.add(c)
                hh, hh_b = hhs[c % 2]
                for gf in range(NTGF):
                    S.dma("sp", s_hh[c % 2], hh[:, 8 + gf * TG:8 + (gf + 1) * TG], x_src(c, gf), [y_b[c][gf]], [hh_b])
                S.op("pool" if OPT_POOL_GPSIMD else "dve", [hh_b, *rstd_b], [hh_b],
                     lambda E: E.tensor_tensor(out=hh[:, 8:8 + NT], in0=hh[:, 8:8 + NT], in1=rstdP, op=ALU.mult))
                S.op("act", [hh_b, mods_b[ls]], [hh_b],
                     lambda E: E.activation(out=hh[:, 8:8 + NT], in_=hh[:, 8:8 + NT], func=AF.Identity,
                                            bias=modB_t[:, ls, c:c + 1], scale=modA_t[:, ls, c:c + 1]))
            xb_b, hal_b, halo_b, e16_b = Buf("xb"), Buf("hal"), Buf("halo"), Buf("e16")
            allx = [y_b[c][g] for c in range(KC) for g in (0, NTGF - 1)]
            src3 = (yT if written[0][0] else xT).rearrange("(c p) t -> p c t", p=128)
            CG = 4
            for c0 in range(0, KC, CG):
                S.dma("sp", s_misc, xb_t[:, c0:c0 + CG, 0:8], src3[:, c0:c0 + CG, 0:8], allx, [xb_b])
                S.dma("sp", s_misc, xb_t[:, c0:c0 + CG, 8:16], src3[:, c0:c0 + CG, NT - 8:NT], allx, [xb_b])

            re_b = Buf("re")

            def emit(E):
                E.tensor_copy(out=re_t[:, 0:8], in_=rstdP[:, 0:8])
                return E.tensor_copy(out=re_t[:, 8:16], in_=rstdP[:, NT - 8:NT])
            S.op("dve", [*rstd_b], [re_b], emit)
            S.op("dve", [xb_b, re_b], [xb_b],
                 lambda E: E.tensor_tensor(out=xb_t[:], in0=xb_t[:], in1=re_t[:].unsqueeze(1).broadcast_to([128, KC, 16]),
                                           op=ALU.mult))
            S.op("dve", [xb_b, mods_b[ls]], [xb_b],
                 lambda E: E.tensor_tensor(out=xb_t[:], in0=xb_t[:], in1=A_.unsqueeze(2).broadcast_to([128, KC, 16]),
                                           op=ALU.mult))
            S.op("dve", [xb_b, mods_b[ls]], [xb_b],
                 lambda E: E.tensor_tensor(out=xb_t[:], in0=xb_t[:], in1=B_.unsqueeze(2).broadcast_to([128, KC, 16]),
                                           op=ALU.add))
            hodram_b, hidram_b = Buf("hodram"), Buf("hidram")
            S.dma("sp", s_misc, halo_out.ap(), xb_t[:].rearrange("p c t -> p (c t)"), [xb_b], [hodram_b])
            S.custom("pool", s_cc[0], 1, [hodram_b], [hidram_b],
                     lambda E: E.collective_compute("AllGather", ALU.bypass,
                                                    replica_groups=[[0, 1], [2, 3], [4, 5], [6, 7]],
                                                    ins=[halo_out.ap().opt()], outs=[halo_in.ap().opt()]))
            chunk_front(0)
            chunk_front(1)
            for s2_ in range(2):
                S.dma("sp", s_misc, hal_t[:, s2_, :, :].rearrange("p c t -> p (c t)"),
                      halo_in.ap()[s2_ * 128:(s2_ + 1) * 128, :], [hidram_b], [hal_b])

            def emit(E):
                E.tensor_scalar(out=halL_t[:], in0=hal_t[:, 0, :, 8:16], scalar1=masks_t[:, 0:1], scalar2=None, op0=ALU.mult)
                return E.tensor_scalar(out=halR_t[:], in0=hal_t[:, 1, :, 0:8], scalar1=masks_t[:, 1:2], scalar2=None,
                                       op0=ALU.mult)
            S.op("dve", [hal_b, consts_b], [halo_b], emit)
            s_hh = [s_tab, s_hh1]
            def group_mm_items(gi):
                pb, pb_b = pbs[gi % 2], pbbs[gi % 2]
                slot, sbuf_, ssem = ringA.next()
                S.dma("pool", ssem, slot[:, 0:ICG * GC], poolw_d[gi], [], [sbuf_], max_dma_last_dim=8192)
                s3 = slot[:, 0:ICG * GC].rearrange("p (i c) -> p i c", i=ICG)
                items = []
                for oc in range(ICG):
                    for gf in range(NTGF):
                        def item(oc=oc, gf=gf):
                            fc = gi * ICG + oc
                            xs, xsb = load_x(fc, gf)
                            bi = 4 + (oc * NTGF + gf) % 4

                            def emit(E):
                                ins = None
                                for ic in range(ICG):
                                    ins = E.matmul(banks[bi][:, 0:TG], lhsT=s3[:, ic, oc * 128:(oc + 1) * 128],
                                                   rhs=pb[:, ic, gf * TG:(gf + 1) * TG], start=(ic == 0),
                                                   stop=(ic == ICG - 1))
                                return ins
                            S.op("pe", [sbuf_, *pb_b], [bankb[bi]], emit)
                            store_x(fc, gf, bi, modG_t[:, ls, fc:fc + 1], xs, xsb, [modsG_b[ls]],
                                    stat=(gf, fc == 0, fc == KC - 1))
                        items.append(item)
                return items

            mm_items = []

            def mm_step():
                if mm_items:
                    mm_items.pop(0)()

            for gi, w in enumerate(POOL_WINDOWS):
                pb, pb_b = pbs[gi % 2], pbbs[gi % 2]
                for ic in range(ICG):
                    c = gi * ICG + ic
                    hh, hh_b = hhs[c % 2]
                    chunk_front(c)

                    def emit(E, c=c, hh=hh):
                        E.tensor_copy(out=hh[:, 0:8], in_=halL_t[:, c, :])
                        return E.tensor_copy(out=hh[:, 8 + NT:16 + NT], in_=halR_t[:, c, :])
                    S.op("dve", [hh_b, halo_b], [hh_b], emit)
                    mm_step()
                    cur, curb, width, step = hh, hh_b, LP, 1
                    dsts = [(sA, sA_b), (sB, sB_b)]
                    di = 0
                    while step < w:
                        dst, dstb = dsts[di % 2]
                        di += 1
                        nw = width - step
                        S.op("dve", [curb], [dstb],
                             lambda E, cur=cur, dst=dst, nw=nw, step=step: E.tensor_tensor(
                                 out=dst[:, 0:nw], in0=cur[:, 0:nw], in1=cur[:, step:step + nw], op=ALU.add))
                        mm_step()
                        cur, curb, width, step = dst, dstb, nw, step * 2
                    o0 = 8 - w // 2
                    S.op("dve", [curb, hh_b], [pb_b[ic]],
                         lambda E, cur=cur, o0=o0, ic=ic, w=w, hh=hh, pb=pb: E.scalar_tensor_tensor(
                             out=pb[:, ic, :], in0=cur[:, o0:o0 + NT], scalar=1.0 / w, in1=hh[:, 8:8 + NT],
                             op0=ALU.mult, op1=ALU.subtract))

                    def emit(E, cur=cur, o0=o0, ic=ic, gi=gi):
                        E.tensor_tensor(out=e16_t[:, 0, :], in0=cur[:, o0:o0 + 8], in1=invc_t[:, gi * 16:gi * 16 + 8], op=ALU.mult)
                        return E.tensor_tensor(out=e16_t[:, 1, :], in0=cur[:, o0 + NT - 8:o0 + NT],
                                               in1=invc_t[:, gi * 16 + 8:gi * 16 + 16], op=ALU.mult)
                    S.op("dve", [curb, consts_b], [e16_b], emit)

                    def emit(E, ic=ic, hh=hh, pb=pb):
                        E.tensor_tensor(out=pb[:, ic, 0:8], in0=e16_t[:, 0, :], in1=hh[:, 8:16], op=ALU.subtract)
                        return E.tensor_tensor(out=pb[:, ic, NT - 8:NT], in0=e16_t[:, 1, :], in1=hh[:, NT:NT + 8],
                                               op=ALU.subtract)
                    S.op("dve", [e16_b, hh_b, pb_b[ic]], [pb_b[ic]], emit)
                    mm_step()
                    mm_step()
                while mm_items:
                    mm_items.pop(0)()
                mm_items.extend(group_mm_items(gi))
            while mm_items:
                mm_items.pop(0)()
            stat_fin([(gf, gf) for gf in range(NTGF)])

        qk_pending = []

        def qk_post(bank_i, gcol, dst_ap, dst_buf, tsl):
            k = sq_n[0] % NSQ
            sq_n[0] += 1
            sqa, sqb = sq_t[:, k, :], sq_b[k]
            n = qt_n[0] % NQT
            m2 = qt_n[0] % 2
            qt_n[0] += 1
            u, t1, t2, rq = u_t[:, n, :], t1_t[:, n, :], t2_t[:, n, :], rq_t[:, n, :]
            ub, t1b, t2b, rqb = (qt_b[x][n] for x in range(4))
            bk = banks[bank_i]
            bm, br = 2 + m2, 4 + m2
            S.op("act", [bankb[bank_i]], [sqb], lambda E: E.activation(out=sqa, in_=bk[:, 0:TG], func=AF.Square))
            S.op("act", [bankb[bank_i], consts_b], [ub],
                 lambda E: E.activation(out=u, in_=bk[:, 0:TG], func=AF.Identity, bias=0.0, scale=qkg_t[:, gcol:gcol + 1]))

            def stage2():
                S.op("pe", [sqb, consts_b], [bankb[bm]],
                     lambda E: E.matmul(banks[bm][:, 0:TG], lhsT=onesH_t[:], rhs=sqa, start=True, stop=True))
                S.op("pe", [ub, consts_b], [bankb[br]],
                     lambda E: E.matmul(banks[br][:, 0:TG], lhsT=pt_t[:], rhs=u, start=True, stop=True))
                S.op("act", [bankb[bm]], [rqb],
                     lambda E: E.activation(out=rq, in_=banks[bm][:, 0:TG], func=AF.Ln, bias=epsc_t[:, 0:1], scale=1.0))
                S.op("act", [rqb], [rqb], lambda E: E.activation(out=rq, in_=rq, func=AF.Exp, bias=0.0, scale=-0.5))
                S.op("dve", [ub, tab_b], [t1b], lambda E: E.tensor_tensor(out=t1, in0=u, in1=tabc_t[:, tsl], op=ALU.mult))
                S.op("dve", [bankb[br], tab_b], [t2b],
                     lambda E: E.tensor_tensor(out=t2, in0=banks[br][:, 0:TG], in1=tabs_t[:, tsl], op=ALU.mult))
                S.op("dve", [t1b, t2b], [t1b], lambda E: E.tensor_tensor(out=t1, in0=t1, in1=t2, op=ALU.add))
                S.op("dve", [t1b, rqb], [dst_buf], lambda E: E.tensor_tensor(out=dst_ap, in0=t1, in1=rq, op=ALU.mult))
            while len(qk_pending) >= 1:
                qk_pending.pop(0)()
            qk_pending.append(stage2)
            if not OPT_QK_PIPE:
                qk_flush()

        def qk_flush():
            while qk_pending:
                qk_pending.pop(0)()

        def load_tables(tile):
            qk_flush()
            S.dma("sp", s_tab, tabc_t[:], cos_d[:, tile * TT:(tile + 1) * TT], [], [tab_b])
            S.dma("sp", s_tab, tabs_t[:], sin_d[:, tile * TT:(tile + 1) * TT], [], [tab_b])

        def proj_fm(s3, off, tg, bi, sbuf_):
            def emit(E):
                ins = None
                for kc in range(KC):
                    ins = E.matmul(banks[bi][:, 0:TG], lhsT=s3[:, kc, off:off + 128],
                                   rhs=h_t[:, kc, tg * TG:(tg + 1) * TG], start=(kc == 0), stop=(kc == KC - 1))
                return ins
            S.op("pe", [sbuf_, *h_b], [bankb[bi]], emit)

        def attention(ls, j_):
            kst_b, vst_b = r_claim(["kstage", "vstage"])
            kst = R_t[:, 0:NKV * NT].rearrange("p (k t) -> p k t", k=NKV)
            vst = R_t[:, NKV * NT:NKV * NT + NB * NKV * 128].rearrange("p (b e) -> p b e", b=NB)
            pj = [0]
            for tile in range(NTILE):
                norm_h(ls, tile)
                load_tables(tile)
                for rk in range(cfg.NKC):
                    slot, sbuf_, ssem = ringA.next()
                    S.dma("pool", ssem, slot, wk_d[rk], [], [sbuf_], max_dma_last_dim=8192)
                    s3 = slot.rearrange("p (k c) -> p k c", k=KC)
                    for hd in range(min(2, NKV - rk * 2)):
                        kh = rk * 2 + hd
                        for tg in range(NTG):
                            bi = pj[0] % 2
                            pj[0] += 1
                            proj_fm(s3, hd * 128, tg, bi, sbuf_)
                            t0 = tile * TT + tg * TG
                            qk_post(bi, 1, kst[:, kh, t0:t0 + TG], kst_b, slice(tg * TG, (tg + 1) * TG))
                qk_flush()
                for rv in range(cfg.NVC):
                    slot, sbuf_, ssem = ringA.next()
                    S.dma("pool", ssem, slot, wv_d[rv], [], [sbuf_], max_dma_last_dim=8192)
                    s3 = slot.rearrange("p (k c) -> p k c", k=KC)
                    ncol = min(2, NKV - rv * 2) * 128
                    for tb in range(TT // 128):
                        bi = pj[0] % 2
                        pj[0] += 1

                        def emit(E, bi=bi, s3=s3, tb=tb, ncol=ncol):
                            ins = None
                            for kc in range(KC):
                                ins = E.matmul(banks[bi][:, 0:ncol], lhsT=h_t[:, kc, tb * 128:(tb + 1) * 128],
                                               rhs=s3[:, kc, 0:ncol], start=(kc == 0), stop=(kc == KC - 1))
                            return ins
                        S.op("pe", [sbuf_, *h_b], [bankb[bi]], emit)
                        blk = tile * (TT // 128) + tb
                        S.op("act", [bankb[bi]], [vst_b],
                             lambda E, bi=bi, blk=blk, rv=rv, ncol=ncol: E.activation(
                                 out=vst[:, blk, rv * 256:rv * 256 + ncol], in_=banks[bi][:, 0:ncol], func=AF.Copy))
                pump_mods(2)
            qk_flush()
            kod_b, vod_b, kid_b, vid_b = Buf("kod"), Buf("vod"), Buf("kid"), Buf("vid")
            S.dma("sp", s_kv[0], k_out.ap().rearrange("(k d) t -> d k t", d=128), kst, [kst_b], [kod_b])
            S.dma("sp", s_kv[1], v_out.ap().rearrange("(b p) e -> p b e", p=128), vst, [vst_b], [vod_b])
            for (o_, i_, ob, ib, sc_) in ((k_out, k_in, kod_b, kid_b, s_cc[1]), (v_out, v_in, vod_b, vid_b, s_cc[2])):
                S.custom("pool", sc_, 1, [ob], [ib],
                         lambda E, o_=o_, i_=i_: E.collective_compute(
                             "AllGather", ALU.bypass, replica_groups=[[0, 1], [2, 3], [4, 5], [6, 7]],
                             ins=[o_.ap().opt()], outs=[i_.ap().opt()]))
            names = [f"Q{hq}" for hq in range(NH)] + ["kvg0", "kvg1"]
            rb = r_claim(names)
            Q_b, kvg_b = rb[:NH], rb[NH:]
            Q_t = R_t[:, 0:NH * TT].rearrange("p (h t) -> p h t", h=NH)
            o_kv = NH * TT
            kvsz = 2 * NT + 2 * NB * 128
            s_kvg = [S.sem("d_kvg0"), S.sem("d_kvg1")]
            kin4 = k_in.ap().rearrange("(s k d) t -> k d s t", s=2, d=128)
            vin3 = v_in.ap().rearrange("(b p) e -> p b e", p=128)
            kvn = [0]
            for tile in range(NTILE):
                norm_h(ls, tile)
                load_tables(tile)
                for rq_ in range(cfg.NQC):
                    slot, sbuf_, ssem = ringA.next()
                    S.dma("pool", ssem, slot, wq_d[rq_], [], [sbuf_], max_dma_last_dim=8192)
                    s3 = slot.rearrange("p (k c) -> p k c", k=KC)
                    for hd in range(min(2, NH - rq_ * 2)):
                        hq = rq_ * 2 + hd
                        for tg in range(NTG):
                            bi = pj[0] % 2
                            pj[0] += 1
                            proj_fm(s3, hd * 128, tg, bi, sbuf_)
                            qk_post(bi, 0, Q_t[:, hq, tg * TG:(tg + 1) * TG], Q_b[hq], slice(tg * TG, (tg + 1) * TG))
                    pump_mods(1)
                qk_flush()
                for kh in range(NKV):
                    kk = kvn[0] % 2
                    kvn[0] += 1
                    base = o_kv + kk * kvsz
                    Kg = R_t[:, base:base + 2 * NT].rearrange("p (s t) -> p s t", s=2)
                    Vg = R_t[:, base + 2 * NT:base + kvsz].rearrange("p (b e) -> p b e", e=128)
                    S.dma("sp", s_kvg[kk], Kg, kin4[kh], [kid_b], [kvg_b[kk]])
                    for b0 in range(0, 2 * NB, 8):
                        S.dma("sp", s_kvg[kk], Vg[:, b0:b0 + 8, :], vin3[:, b0:b0 + 8, kh * 128:(kh + 1) * 128], [vid_b],
                              [kvg_b[kk]])
                    for hq in range(kh * GQ, (kh + 1) * GQ):
                        for qg in range(NTG):
                            qsl = slice(qg * TG, (qg + 1) * TG)
                            bO, bZ = 4 + (hq * NTG + qg) % 2, 6 + (hq * NTG + qg) % 2
                            NKB = 2 * NB

                            NP = NKB // 2

                            def s_pair(j, Kg=Kg, hq=hq, qsl=qsl, kk=kk):
                                pb_ = (j % 2) * 2
                                rhs = Q_t[:, hq, qsl]

                                def emit(E):
                                    ins = None
                                    for t_ in range(2):
                                        kb = 2 * j + t_
                                        sl_, kbl = divmod(kb, NB)
                                        ins = E.matmul(psum_all[:, pb_ * 512 + t_ * TG:pb_ * 512 + (t_ + 1) * TG],
                                                       lhsT=Kg[:, sl_, kbl * 128:(kbl + 1) * 128], rhs=rhs, start=True, stop=True)
                                    return ins
                                S.op("pe", [kvg_b[kk], Q_b[hq]], [bankb[pb_], bankb[pb_ + 1]], emit)
                            s_pair(0)
                            if NP > 1:
                                s_pair(1)
                            for j in range(NP):
                                pb_ = (j % 2) * 2
                                sl_ = (2 * j) // NB
                                ke = e_n[0] % NE
                                e_n[0] += 1
                                ea, eb = e_t[:, ke, :], e_b[ke]
                                S.op("act", [bankb[pb_], bankb[pb_ + 1], consts_b], [eb],
                                     lambda E, pb_=pb_, ea=ea, sl_=sl_: E.activation(
                                         out=ea, in_=psum_all[:, pb_ * 512:pb_ * 512 + 2 * TG], func=AF.Exp,
                                         bias=masks_t[:, 2 + sl_:3 + sl_], scale=SM_SCALE))
                                if j + 2 < NP:
                                    s_pair(j + 2)

                                def emit(E, j=j, ea=ea, bO=bO, Vg=Vg, NKB=NKB):
                                    ins = None
                                    for t_ in range(2):
                                        kb = 2 * j + t_
                                        et = ea[:, t_ * TG:(t_ + 1) * TG]
                                        ins = E.matmul(banks[bO][:, 0:TG], lhsT=Vg[:, kb, :], rhs=et, start=(kb == 0),
                                                       stop=(kb == NKB - 1))
                                    return ins
                                S.op("pe", [eb, kvg_b[kk]], [bankb[bO]], emit)
                                e0, e1 = ea[:, 0:TG], ea[:, TG:2 * TG]
                                kq = zt_n[0] % 2
                                zt_n[0] += 1
                                zsa, zsb = zs_t[:, kq, :], zs_b[kq]
                                S.op("dve", [eb], [zsb],
                                     lambda E, zsa=zsa, e0=e0, e1=e1: E.tensor_tensor(out=zsa, in0=e0, in1=e1, op=ALU.add))
                                while zq_pend:
                                    zq_pend.pop(0)()
                                zq_pend.append(lambda j=j, zsa=zsa, zsb=zsb, bZ=bZ, NP=NP: S.op(
                                    "pe", [zsb, consts_b], [bankb[bZ]],
                                    lambda E: E.matmul(banks[bZ][:, 0:TG], lhsT=ones1_t[:], rhs=zsa, start=(j == 0),
                                                       stop=(j == NP - 1))))
                            while zq_pend:
                                zq_pend.pop(0)()
                            kz = rz_n[0] % 2
                            rz_n[0] += 1
                            rza, rzb = rz_t[:, kz, :], rz_b[kz]
                            S.op("dve", [bankb[bZ]], [rzb],
                                 lambda E, rza=rza, bZ=bZ: E.reciprocal(out=rza, in_=banks[bZ][:, 0:TG]))
                            S.op("dve", [bankb[bO], rzb], [h_b[hq]],
                                 lambda E, rza=rza, hq=hq, qsl=qsl, bO=bO: E.tensor_tensor(
                                     out=h_t[:, hq, qsl], in0=banks[bO][:, 0:TG], in1=rza, op=ALU.mult))
                for fc in range(KC):
                    slot, sbuf_, ssem = ringB.next()
                    S.dma("pool", ssem, slot[:, 0:NH * 128], wo_d[fc], [], [sbuf_], max_dma_last_dim=8192)
                    s3 = slot[:, 0:NH * 128].rearrange("p (h c) -> p h c", h=NH)
                    for tg in range(NTG):
                        gf = tile * NTG + tg
                        xs, xsb = load_x(fc, gf)
                        bi = (fc * NTG + tg) % 2

                        def emit(E, bi=bi, s3=s3, tg=tg):
                            ins = None
                            for hq in range(NH):
                                ins = E.matmul(banks[bi][:, 0:TG], lhsT=s3[:, hq, :], rhs=h_t[:, hq, tg * TG:(tg + 1) * TG],
                                               start=(hq == 0), stop=(hq == NH - 1))
                            return ins
                        S.op("pe", [sbuf_, *h_b], [bankb[bi]], emit)
                        st_ = (6 + tg, fc == 0, fc == KC - 1) if ls + 1 < NSUB else None
                        store_x(fc, gf, bi, modG_t[:, ls, fc:fc + 1], xs, xsb, [modsG_b[ls]], stat=st_)
                if ls + 1 < NSUB:
                    stat_fin([(6 + tg, tile * NTG + tg) for tg in range(NTG)])

        schedule_mods(0)
        norm_stats(0, rstd_t[:, 0:TT], rstd_b[0], [6, 7])
        stats0_done[0] = True
        pump_mods(2 * RSUB // 3 + 1)
        mods_sched = [1]

        def is_ffn(l_):
            return l_ % 3 != 1
        for i in range(L):
            for s_ in range(3):
                ls = i * 3 + s_
                if is_ffn(ls):
                    nf = next((l_ for l_ in range(ls + 1, NSUB) if is_ffn(l_)), NSUB - 1)
                    while mods_sched[0] <= nf and mods_sched[0] < NSUB:
                        schedule_mods(mods_sched[0])
                        mods_sched[0] += 1
                if s_ == 0:
                    ffn(ls, i, 0)
                elif s_ == 2:
                    ffn(ls, i, 1)
                elif i % 2 == 0:
                    pool_mixer(ls, i // 2)
                else:
                    attention(ls, i // 2)
                pump_mods()
        S.wait_all("sp", [b for row in y_b for b in row])

        _LAST_SCHED[0] = S
        block = es.enter_context(nc.Block())

        @block.tensor
        def _(E):
            for f in S.q["pe"]:
                f(E)

        @block.scalar
        def _(E):
            for f in S.q["act"]:
                f(E)

        @block.vector
        def _(E):
            for f in S.q["dve"]:
                f(E)

        @block.gpsimd
        def _(E):
            for f in S.q["pool"]:
                f(E)

        @block.sync
        def _(E):
            for f in S.q["sp"]:
                f(E)
    return nc


def _ring_tiles(w, KC):
    K, ncols = w.shape
    r = ncols // 256
    return np.ascontiguousarray(w.reshape(KC, 128, r, 256).transpose(2, 1, 0, 3)).reshape(r, 128, KC * 256)


def _pad_cols(w, mult=256):
    K, n = w.shape
    if n % mult == 0:
        return w
    out = np.zeros((K, (n + mult - 1) // mult * mult), w.dtype)
    out[:, :n] = w
    return out


def _rope_tables(cfg, t0):
    t = np.arange(t0, t0 + cfg.NT)
    r = (t // cfg.GRID_W).astype(np.float32)
    c = (t % cfg.GRID_W).astype(np.float32)
    inv = (ROPE_THETA ** (-np.arange(0, 64, 2, dtype=np.float32) / np.float32(64))).astype(np.float32)
    ang_r = (r[:, None] * inv[None, :]).astype(np.float32)
    ang_c = (c[:, None] * inv[None, :]).astype(np.float32)
    cosT = np.empty((128, cfg.NT), np.float32)
    sinS = np.empty((128, cfg.NT), np.float32)
    for base, ang in ((0, ang_r), (64, ang_c)):
        cs, sn = np.cos(ang).T.astype(np.float32), np.sin(ang).T.astype(np.float32)
        cosT[base:base + 32] = cs
        cosT[base + 32:base + 64] = cs
        sinS[base:base + 32] = -sn
        sinS[base + 32:base + 64] = sn
    return cosT, sinS


def _perm_matrix():
    pm = np.zeros((128, 128), np.float32)
    for m in range(128):
        k = m + 32 if (m % 64) < 32 else m - 32
        pm[k, m] = 1.0
    return pm


def _invc(cfg, t0, S):
    out = np.zeros((4, 16), np.float32)
    loc = list(range(8)) + list(range(cfg.NT - 8, cfg.NT))
    for gi, w in enumerate(POOL_WINDOWS):
        for n, tl in enumerate(loc):
            t = t0 + tl
            lo, hi = max(t - w // 2, 0), min(t + w // 2, S)
            out[gi, n] = 1.0 / float(hi - lo)
    return np.ascontiguousarray(np.broadcast_to(out.reshape(1, 64), (128, 64)))


def host_prepare(cfg, x_prompt, x_sample, c_prompt, c_sample, ada_w, ada_b, norm_g, ffn_w_in, ffn_w_out,
                 pool_w, pool_scale, attn_w_qkv, attn_q_g, attn_k_g, attn_w_o):
    KC, D, NT, NJ, NJH, NH, NKV, L = cfg.KC, cfg.D, cfg.NT, cfg.NJ, cfg.NJH, cfg.NH, cfg.NKV, cfg.L
    f = lambda a: np.asarray(a, dtype=np.float32)
    x_prompt, x_sample, c_prompt, c_sample = f(x_prompt), f(x_sample), f(c_prompt), f(c_sample)
    ada_w, ada_b, norm_g, ffn_w_in, ffn_w_out = f(ada_w), f(ada_b), f(norm_g), f(ffn_w_in), f(ffn_w_out)
    pool_w, pool_scale, attn_w_qkv, attn_q_g, attn_k_g, attn_w_o = (f(pool_w), f(pool_scale), f(attn_w_qkv),
                                                                    f(attn_q_g), f(attn_k_g), f(attn_w_o))
    NSUB = 3 * L
    sh = {}
    sh["ada_w"] = np.stack([_ring_tiles(ada_w[i], KC) for i in range(L)])
    sh["ada_b"] = np.ascontiguousarray(ada_b.reshape(L * 3 * 3 * KC, 128).T)
    sh["norm_g"] = np.ascontiguousarray(norm_g.reshape(NSUB * KC, 128).T)
    win = np.empty((L, 2, NJ, 128, KC * 256), np.float32)
    wout = np.empty((L, 2, 2, KC, 128, NJH * 128), np.float32)
    for i in range(L):
        for w_ in range(2):
            wi = ffn_w_in[i, w_]
            g = wi[:, :cfg.DFF].reshape(KC, 128, NJ, 128)
            u = wi[:, cfg.DFF:].reshape(KC, 128, NJ, 128)
            gu = np.concatenate([g, u], axis=3)
            win[i, w_] = gu.transpose(2, 1, 0, 3).reshape(NJ, 128, KC * 256)
            wo = ffn_w_out[i, w_].reshape(2, NJH, 128, KC, 128)
            wout[i, w_] = wo.transpose(0, 3, 2, 1, 4).reshape(2, KC, 128, NJH * 128)
    sh["w_in"], sh["w_out"] = win, wout
    pw = pool_w[0].reshape(4, cfg.ICG, 128, cfg.GC)
    sh["pool_w"] = np.ascontiguousarray(pw.transpose(0, 2, 1, 3)).reshape(4, 128, cfg.ICG * cfg.GC)
    sh["pool_scale"] = np.ascontiguousarray(pool_scale[0].reshape(KC, 128).T)
    wqkv = attn_w_qkv[0]
    sh["w_q"] = _ring_tiles(_pad_cols(wqkv[:, :NH * 128]), KC)
    sh["w_k"] = _ring_tiles(_pad_cols(wqkv[:, NH * 128:(NH + NKV) * 128]), KC)
    sh["w_v"] = _ring_tiles(_pad_cols(wqkv[:, (NH + NKV) * 128:]), KC)
    sh["qkg"] = np.ascontiguousarray(np.stack([attn_q_g[0], attn_k_g[0]], axis=1))
    wo = attn_w_o[0].reshape(NH, 128, KC, 128)
    sh["w_o"] = np.ascontiguousarray(wo.transpose(2, 1, 0, 3)).reshape(KC, 128, NH * 128)
    sh["PT"] = _perm_matrix()
    in_maps = []
    for core in range(8):
        if core < 4:
            b, half = divmod(core, 2)
            xs = x_prompt[b, half * NT:(half + 1) * NT]
            cv = c_prompt[b]
            t0, Sq = half * NT, 2 * NT
            lm, rm = (0.0, 1.0) if half == 0 else (1.0, 0.0)
            kb0, kb1 = 0.0, 0.0
        else:
            b = core - 4
            xs = x_sample[b]
            cv = c_sample[b]
            t0, Sq = 0, NT
            lm, rm = 0.0, 0.0
            kb0, kb1 = (0.0, -30000.0) if core % 2 == 0 else (-30000.0, 0.0)
        m = dict(sh)
        m["xT"] = np.ascontiguousarray(xs.T)
        m["cvec"] = np.ascontiguousarray(cv.reshape(KC, 128).T)
        m["cosT"], m["sinS"] = _rope_tables(cfg, t0)
        m["invc"] = _invc(cfg, t0, Sq)
        m["masks"] = np.ascontiguousarray(np.broadcast_to(np.array([[lm, rm, kb0, kb1]], np.float32), (128, 4)))
        in_maps.append(m)
    return in_maps


_PROG_CACHE = {}


def run(cfg, **inputs):
    in_maps = host_prepare(cfg, **inputs)
    key = id(cfg)
    if key not in _PROG_CACHE:
        _PROG_CACHE[key] = build_program(cfg)
    nc = _PROG_CACHE[key]
    res = run_bass_kernel_spmd(nc, in_maps, core_ids=list(range(8)))
    NT, D = cfg.NT, cfg.D
    yp = np.empty((2, 2 * NT, D), np.float32)
    ys = np.empty((4, NT, D), np.float32)
    for core in range(8):
        y = np.asarray(res.results[core]["yT"]).T
        if core < 4:
            b, half = divmod(core, 2)
            yp[b, half * NT:(half + 1) * NT] = y
        else:
            ys[core - 4] = y
    return yp, ys


def kernel(**inputs):
    return run(CFG_FULL, **inputs)
```
